# Optimizing a Trainium2 kernel written in Bass

```python
import math
import jax, jax.numpy as jnp
from jax import lax
import numpy as np

D_MODEL = 2048
BATCH = 2
SEQ = 4096
DEPTH = 1

HEAD_DIM = 128
D_MIX = D_MODEL
FOX_HEADS = 8
NSA_HEADS = 8
NSA_KV_HEADS = 2
NSA_GROUP = NSA_HEADS // NSA_KV_HEADS
CMP_LEN = 32
CMP_STRIDE = 16
CMP_HIDDEN = 2 * HEAD_DIM
SLC_LEN = 64
SLC_TOPK = 16
WINDOW = 512
Q_BLOCK = 128
D_FF = ((8 * D_MODEL // 3 + 255) // 256) * 256
ROPE_THETA = 10000.0
NORM_EPS = 1e-6
MASK_VALUE = -1e30
SEL_BONUS = 1e6
D_IN = (3 * FOX_HEADS * HEAD_DIM + FOX_HEADS + NSA_HEADS * HEAD_DIM
        + 6 * NSA_KV_HEADS * HEAD_DIM + 3 * NSA_HEADS)

kernel_name = "hymba_fox_nsa_adaln_block"


def _rmsnorm(x, g):
    xf = x.astype(jnp.float32)
    y = xf * lax.rsqrt(jnp.mean(xf * xf, axis=-1, keepdims=True) + NORM_EPS)
    return y.astype(x.dtype) * g


def _modulate(h, shift, scale):
    return h * (1 + scale[:, None, :]) + shift[:, None, :]


def _rope(x, pos):
    d = x.shape[-1]
    inv_freq = ROPE_THETA ** (-jnp.arange(0, d, 2, dtype=jnp.float32) / d)
    ang = pos.astype(jnp.float32)[..., None] * inv_freq
    cos = jnp.cos(ang)[:, :, None, :].astype(x.dtype)
    sin = jnp.sin(ang)[:, :, None, :].astype(x.dtype)
    x1, x2 = jnp.split(x, 2, axis=-1)
    return jnp.concatenate([x1 * cos - x2 * sin, x2 * cos + x1 * sin], axis=-1)


def _masked_softmax(s, mask):
    s = jnp.where(mask, s.astype(jnp.float32), MASK_VALUE)
    m = jnp.max(s, axis=-1, keepdims=True)
    e = jnp.where(mask, jnp.exp(s - m), 0.0)
    return e / jnp.maximum(jnp.sum(e, axis=-1, keepdims=True), 1e-30)


def _split_points():
    sizes = ([FOX_HEADS * HEAD_DIM] * 3 + [FOX_HEADS] + [NSA_HEADS * HEAD_DIM]
             + [NSA_KV_HEADS * HEAD_DIM] * 6 + [3 * NSA_HEADS])
    points, acc = [], 0
    for s in sizes[:-1]:
        acc += s
        points.append(acc)
    return points


def _fox_attention(q, k, v, log_f):
    B, S, H, D = q.shape
    cum = jnp.cumsum(log_f, axis=1).transpose(0, 2, 1)
    kpos = jnp.arange(S)
    scale = D ** -0.5

    def block(i):
        start = i * Q_BLOCK
        qb = lax.dynamic_slice_in_dim(q, start, Q_BLOCK, axis=1)
        cb = lax.dynamic_slice_in_dim(cum, start, Q_BLOCK, axis=2)
        qpos = start + jnp.arange(Q_BLOCK)
        s = jnp.einsum('bqhd,bkhd->bhqk', qb, k).astype(jnp.float32) * scale
        s = s + (cb[..., :, None] - cum[..., None, :])
        mask = (kpos[None, :] <= qpos[:, None])[None, None]
        p = _masked_softmax(s, mask)
        return jnp.einsum('bhqk,bkhd->bqhd', p.astype(v.dtype), v)

    out = lax.map(block, jnp.arange(S // Q_BLOCK))
    return out.transpose(1, 0, 2, 3, 4).reshape(B, S, H, D)


def _compress(x, cmp_pos, w1, w2, idx):
    B, _, G, D = x.shape
    blk = x[:, idx] + cmp_pos[None, None, :, None, :]
    blk = blk.transpose(0, 1, 3, 2, 4).reshape(B, idx.shape[0], G, CMP_LEN * D)
    return jax.nn.gelu(blk @ w1) @ w2


def _nsa_attention(q, k_c, v_c, k_s, v_s, k_w, v_w, gates, positions,
                   cmp_pos, w_kc1, w_kc2, w_vc1, w_vc2):
    B, S, H, D = q.shape
    G, R = NSA_KV_HEADS, NSA_GROUP
    scale = D ** -0.5
    q = _rope(q, positions)
    k_s = _rope(k_s, positions)
    k_w = _rope(k_w, positions)

    n_cmp = (S - CMP_LEN) // CMP_STRIDE + 1
    cmp_idx = jnp.arange(n_cmp)[:, None] * CMP_STRIDE + jnp.arange(CMP_LEN)[None, :]
    cmp_end = cmp_idx[:, -1]
    kc = _rope(_compress(k_c, cmp_pos, w_kc1, w_kc2, cmp_idx), positions[:, cmp_end])
    vc = _compress(v_c, cmp_pos, w_vc1, w_vc2, cmp_idx)

    n_slc = S // SLC_LEN
    n_sel = min(SLC_TOPK, n_slc)
    cs = jnp.arange(n_cmp) * CMP_STRIDE
    ss = jnp.arange(n_slc) * SLC_LEN
    ov = (jnp.minimum(cs[:, None] + CMP_LEN, ss[None, :] + SLC_LEN)
          - jnp.maximum(cs[:, None], ss[None, :]))
    overlap = jnp.clip(ov, 0).astype(jnp.float32) / CMP_STRIDE

    k_sb = k_s.reshape(B, n_slc, SLC_LEN, G, D).transpose(0, 3, 1, 2, 4)
    v_sb = v_s.reshape(B, n_slc, SLC_LEN, G, D).transpose(0, 3, 1, 2, 4)
    pad = ((0, 0), (WINDOW, 0), (0, 0), (0, 0))
    k_wp = jnp.pad(k_w, pad)
    v_wp = jnp.pad(v_w, pad)
    q_g = q.reshape(B, S, G, R, D)
    gates_g = gates.reshape(B, S, G, R, 3)
    b_idx = jnp.arange(B)[:, None, None, None]
    g_idx = jnp.arange(G)[None, :, None, None]
    blk_ids = jnp.arange(n_slc)

    def block(i):
        start = i * Q_BLOCK
        qb = lax.dynamic_slice_in_dim(q_g, start, Q_BLOCK, axis=1)
        qpos = start + jnp.arange(Q_BLOCK)

        s_c = jnp.einsum('bqgrd,bcgd->bgrqc', qb, kc) * scale
        mask_c = (cmp_end[None, :] <= qpos[:, None])[None, None, None]
        p_c = _masked_softmax(s_c, mask_c)
        o_c = jnp.einsum('bgrqc,bcgd->bqgrd', p_c.astype(vc.dtype), vc)

        imp = jnp.einsum('bgrqc,cn->bgqn', p_c, overlap)
        cur = qpos // SLC_LEN
        forced = ((blk_ids[None, :] == 0) | (blk_ids[None, :] == cur[:, None])
                  | (blk_ids[None, :] == cur[:, None] - 1))
        causal_blk = blk_ids[None, :] <= cur[:, None]
        score = jnp.where(forced[None, None], SEL_BONUS, imp)
        score = jnp.where(causal_blk[None, None], score, -SEL_BONUS)
        _, sel = lax.top_k(score, n_sel)
        ks_sel = k_sb[b_idx, g_idx, sel].reshape(B, G, Q_BLOCK, n_sel * SLC_LEN, D)
        vs_sel = v_sb[b_idx, g_idx, sel].reshape(B, G, Q_BLOCK, n_sel * SLC_LEN, D)
        tok = (sel[..., None] * SLC_LEN + jnp.arange(SLC_LEN)).reshape(B, G, Q_BLOCK, -1)
        mask_s = (tok <= qpos[None, None, :, None])[:, :, None]
        s_s = jnp.einsum('bqgrd,bgqnd->bgrqn', qb, ks_sel) * scale
        p_s = _masked_softmax(s_s, mask_s)
        o_s = jnp.einsum('bgrqn,bgqnd->bqgrd', p_s.astype(vs_sel.dtype), vs_sel)

        kw = lax.dynamic_slice_in_dim(k_wp, start, Q_BLOCK + WINDOW, axis=1)
        vw = lax.dynamic_slice_in_dim(v_wp, start, Q_BLOCK + WINDOW, axis=1)
        kpos = start - WINDOW + jnp.arange(Q_BLOCK + WINDOW)
        mask_w = ((kpos[None, :] <= qpos[:, None]) & (qpos[:, None] - kpos[None, :] < WINDOW)
                  & (kpos[None, :] >= 0))[None, None, None]
        s_w = jnp.einsum('bqgrd,bkgd->bgrqk', qb, kw) * scale
        p_w = _masked_softmax(s_w, mask_w)
        o_w = jnp.einsum('bgrqk,bkgd->bqgrd', p_w.astype(vw.dtype), vw)

        gb = lax.dynamic_slice_in_dim(gates_g, start, Q_BLOCK, axis=1)
        o = gb[..., 0:1] * o_c + gb[..., 1:2] * o_s + gb[..., 2:3] * o_w
        return o.reshape(B, Q_BLOCK, H, D)

    out = lax.map(block, jnp.arange(S // Q_BLOCK))
    return out.transpose(1, 0, 2, 3, 4).reshape(B, S, H, D)


def _hybrid_mixer(u, positions, w_in, b_fgate, cmp_pos, w_kc1, w_kc2, w_vc1, w_vc2,
                  beta_fox, beta_nsa, w_out):
    B, S, _ = u.shape
    z = u @ w_in
    (q_f, k_f, v_f, f_logit, q_n, k_c, v_c, k_s, v_s, k_w, v_w, g_n) = jnp.split(
        z, _split_points(), axis=-1)
    fh = lambda t: t.reshape(B, S, FOX_HEADS, HEAD_DIM)
    nq = lambda t: t.reshape(B, S, NSA_HEADS, HEAD_DIM)
    nkv = lambda t: t.reshape(B, S, NSA_KV_HEADS, HEAD_DIM)
    log_f = jax.nn.log_sigmoid((f_logit + b_fgate).astype(jnp.float32))
    o_f = _fox_attention(fh(q_f), fh(k_f), fh(v_f), log_f)
    gates = jax.nn.sigmoid(g_n.reshape(B, S, NSA_HEADS, 3))
    o_n = _nsa_attention(nq(q_n), nkv(k_c), nkv(v_c), nkv(k_s), nkv(v_s), nkv(k_w), nkv(v_w),
                         gates, positions, cmp_pos, w_kc1, w_kc2, w_vc1, w_vc2)
    y_f = _rmsnorm(o_f.reshape(B, S, FOX_HEADS * HEAD_DIM), beta_fox)
    y_n = _rmsnorm(o_n.reshape(B, S, NSA_HEADS * HEAD_DIM), beta_nsa)
    return jnp.concatenate([y_f, y_n], axis=-1) @ w_out


def _swiglu(u, w_gate, w_up, w_down):
    return (jax.nn.silu(u @ w_gate) * (u @ w_up)) @ w_down


def setup_inputs(seed: int = 0) -> dict:
    key = jax.random.key(seed)
    ks = jax.random.split(key, 24)
    f32 = jnp.float32
    nrm = lambda k, shape, s: jax.random.normal(k, shape, f32) * s
    L = DEPTH
    x = jax.random.normal(ks[0], (BATCH, SEQ, D_MODEL), f32)
    c = jax.random.normal(ks[1], (BATCH, D_MODEL), f32)
    offset = jax.random.randint(ks[2], (BATCH, 1), 0, 1024, dtype=jnp.int32)
    positions = offset + jnp.arange(SEQ, dtype=jnp.int32)[None, :]
    return {
        "x": x,
        "c": c,
        "positions": positions,
        "w_ada": nrm(ks[3], (L, D_MODEL, 6 * D_MODEL), 0.5 * D_MODEL ** -0.5),
        "b_ada": nrm(ks[4], (L, 6 * D_MODEL), 0.02),
        "norm_attn": 1.0 + nrm(ks[5], (L, D_MODEL), 0.02),
        "norm_ffn": 1.0 + nrm(ks[6], (L, D_MODEL), 0.02),
        "w_in": nrm(ks[7], (L, D_MODEL, D_IN), D_MODEL ** -0.5),
        "b_fgate": 2.0 + nrm(ks[8], (L, FOX_HEADS), 0.5),
        "cmp_pos": nrm(ks[9], (L, CMP_LEN, HEAD_DIM), 0.02),
        "w_kc1": nrm(ks[10], (L, CMP_LEN * HEAD_DIM, CMP_HIDDEN), (CMP_LEN * HEAD_DIM) ** -0.5),
        "w_kc2": nrm(ks[11], (L, CMP_HIDDEN, HEAD_DIM), CMP_HIDDEN ** -0.5),
        "w_vc1": nrm(ks[12], (L, CMP_LEN * HEAD_DIM, CMP_HIDDEN), (CMP_LEN * HEAD_DIM) ** -0.5),
        "w_vc2": nrm(ks[13], (L, CMP_HIDDEN, HEAD_DIM), CMP_HIDDEN ** -0.5),
        "beta_fox": 1.0 + nrm(ks[14], (L, FOX_HEADS * HEAD_DIM), 0.02),
        "beta_nsa": 1.0 + nrm(ks[15], (L, NSA_HEADS * HEAD_DIM), 0.02),
        "w_out": nrm(ks[16], (L, D_MIX, D_MODEL), D_MIX ** -0.5),
        "w_gate": nrm(ks[17], (L, D_MODEL, D_FF), D_MODEL ** -0.5),
        "w_up": nrm(ks[18], (L, D_MODEL, D_FF), D_MODEL ** -0.5),
        "w_down": nrm(ks[19], (L, D_FF, D_MODEL), D_FF ** -0.5),
        "final_norm": 1.0 + nrm(ks[20], (D_MODEL,), 0.02),
    }


def reference(x, c, positions, w_ada, b_ada, norm_attn, norm_ffn, w_in, b_fgate, cmp_pos,
              w_kc1, w_kc2, w_vc1, w_vc2, beta_fox, beta_nsa, w_out, w_gate, w_up, w_down,
              final_norm):
    h = x
    for l in range(DEPTH):
        mod = jax.nn.silu(c) @ w_ada[l] + b_ada[l]
        sh1, sc1, g1, sh2, sc2, g2 = jnp.split(mod, 6, axis=-1)
        u = _modulate(_rmsnorm(h, norm_attn[l]), sh1, sc1)
        y = _hybrid_mixer(u, positions, w_in[l], b_fgate[l], cmp_pos[l], w_kc1[l], w_kc2[l],
                          w_vc1[l], w_vc2[l], beta_fox[l], beta_nsa[l], w_out[l])
        h = h + g1[:, None, :] * y
        u = _modulate(_rmsnorm(h, norm_ffn[l]), sh2, sc2)
        h = h + g2[:, None, :] * _swiglu(u, w_gate[l], w_up[l], w_down[l])
    return _rmsnorm(h, final_norm)
```

```python
import numpy as np
import concourse.bass as bass
import concourse.mybir as mybir
from concourse.bass_utils import run_bass_kernel_spmd

F32 = mybir.dt.float32
BF16 = mybir.dt.bfloat16
I32 = mybir.dt.int32
U8 = mybir.dt.uint8
AF = mybir.ActivationFunctionType
ALU = mybir.AluOpType

D = 2048
S = 4096
NT = 32
DFF = 5632
NFC = DFF // 128
EPS = 1e-6
SCALE = 128 ** -0.5
BIG = 30000.0
NFOX = 770
NNSA_FM = 1024
NNSA_TM = 262
NNSA = NNSA_FM + NNSA_TM
TWO_PI = 2.0 * np.pi


class Sem:
    def __init__(self, nc, name, group=False):
        self.h = nc.alloc_semaphore(name)
        self.count = 0
        self.group = group


class Sched:
    ENG = ("pe", "act", "dve", "pool", "sp")

    def __init__(self, nc):
        self.nc = nc
        self.streams = {e: [] for e in self.ENG}
        self.esem = {e: Sem(nc, "es_" + e) for e in self.ENG}
        self.waited = {e: {} for e in self.ENG}
        self.lastw = {}
        self.lastr = {}
        self.all_sems = list(self.esem.values())
        self.want_pid = False
        self.pid = None

    def new_sem(self, name, group=False):
        s = Sem(self.nc, name, group)
        self.all_sems.append(s)
        return s

    def op(self, eng, fn, reads=(), writes=(), dsem=None, signal=True, inc=None):
        waits = {}

        def need(sc):
            s, c = sc
            if s.group:
                c = s.count
            if c > waits.get(s, 0):
                waits[s] = c
        own0 = self.esem[eng]
        for k in reads:
            w = self.lastw.get(k)
            if w:
                need(w)
            if isinstance(k, tuple) and k[0] == "ps":
                for s, c in self.lastr.get(k, {}).items():
                    if s is not own0:
                        need((s, c))
        for k in writes:
            w = self.lastw.get(k)
            if w:
                need(w)
            for s, c in self.lastr.get(k, {}).items():
                need((s, c))
        own = self.esem[eng]
        final = []
        for s, c in waits.items():
            if s is own and eng == "pe":
                continue
            if self.waited[eng].get(s, 0) >= c:
                continue
            self.waited[eng][s] = c
            final.append((s, c))
        if dsem is not None:
            step = 16 if inc is None else inc
            dsem.count += step
            mark = (dsem, dsem.count)
            sig = (dsem, step)
        else:
            if signal:
                own.count += 1
                mark = (own, own.count)
                sig = (own, 1)
            else:
                mark = (own, own.count + 1)
                sig = None
        for k in writes:
            self.lastw[k] = mark
            self.lastr[k] = {}
        for k in reads:
            d = self.lastr.setdefault(k, {})
            if d.get(mark[0], 0) < mark[1]:
                d[mark[0]] = mark[1]
        self.streams[eng].append((final, fn, sig))

    def barrier(self):
        snap = [(s, s.count) for s in self.all_sems if s.count > 0]
        for e in self.ENG:
            final = []
            for s, c in snap:
                if s is self.esem[e]:
                    continue
                if self.waited[e].get(s, 0) >= c:
                    continue
                self.waited[e][s] = c
                final.append((s, c))
            if final:
                self.streams[e].append((final, None, None))
        self.lastw = {}
        self.lastr = {}

    def replay(self, block):
        def run(engh, stream):
            for waits, fn, sig in stream:
                for s, c in waits:
                    engh.wait_ge(s.h, c)
                if fn is None:
                    continue
                ins = fn(engh)
                if sig is not None:
                    ins.then_inc(sig[0].h, sig[1])
        st = self.streams

        @block.tensor
        def _(e):
            run(e, st["pe"])

        @block.scalar
        def _(e):
            run(e, st["act"])

        @block.vector
        def _(e):
            run(e, st["dve"])

        @block.gpsimd
        def _(e):
            if self.want_pid:
                self.pid = e.partition_id()
                self.rank_row = e.snap((self.pid % 4) * S, min_val=0, max_val=3 * S)
            run(e, st["pool"])

        @block.sync
        def _(e):
            run(e, st["sp"])


class Mem:
    def __init__(self, nc, total):
        self.raw = nc.alloc_sbuf_tensor("raw", [128, total], U8)
        self.total = total
        self.off = 0

    def at(self, off, shape, dt):
        save = self.off
        self.off = off
        v = self.t(shape, dt)
        self.off = save
        return v

    def t(self, shape, dt, parts=128):
        n = 1
        for s in shape:
            n *= s
        nb = n * (2 if dt == BF16 else 4)
        nb_al = (nb + 63) // 64 * 64
        assert self.off + nb_al <= self.total, ("SBUF overflow", self.off, nb_al, self.total)
        v = self.raw[:, self.off:self.off + nb].bitcast(dt)
        self.off += nb_al
        if len(shape) == 2:
            v = v.rearrange("p (a b) -> p a b", a=shape[0])
        elif len(shape) == 3:
            v = v.rearrange("p (a b c) -> p a b c", a=shape[0], b=shape[1])
        elif len(shape) == 4:
            v = v.rearrange("p (a b c d) -> p a b c d", a=shape[0], b=shape[1], c=shape[2])
        return v


def build(stage=99):
    nc = bass.Bass("TRN2", target_bir_lowering=False)
    sc = Sched(nc)

    in_names = []
    NSA_ONLY = ("pos_b", "pos_end", "w_nsa", "cposT", "w_kc1", "w_kc2", "w_vc1", "w_vc2", "c_perm", "c_atri", "c_E", "c_maskc",
                "c_selA", "c_selB", "c_ovl", "c_freq")
    FFN_ONLY = ("x_my", "fnorm", "w_out", "w_gate", "w_up", "w_down")

    def din(name, shape, dt=F32):
        if (name in NSA_ONLY and stage < 2) or (name in FFN_ONLY and stage < 3):
            return None
        if name in ("w_gate", "w_up") and stage < 3.2:
            return None
        if name == "w_down" and stage < 3.3:
            return None
        in_names.append(name)
        return nc.dram_tensor(name, list(shape), dt, kind="ExternalInput").ap()

    x_b = din("x_b", [S, D])
    x_my = din("x_my", [1024, D])
    c_t = din("c_t", [128, 16])
    pos_b = din("pos_b", [1, S], I32)
    pos_end = din("pos_end", [1, 256], I32)
    w_ada = din("w_ada", [D, 3072])
    b_ada = din("b_ada", [1, 3072])
    nattn_t = din("nattn_t", [128, 16])
    nffn_t = din("nffn_t", [128, 16])
    fnorm = din("fnorm", [1, D])
    w_fox = din("w_fox", [D, NFOX])
    b_fg = din("b_fg", [1, 2])
    w_nsa = din("w_nsa", [D, NNSA])
    cposT = din("cposT", [128, 32])
    w_kc1 = din("w_kc1", [4096, 256])
    w_kc2 = din("w_kc2", [256, 128])
    w_vc1 = din("w_vc1", [4096, 256])
    w_vc2 = din("w_vc2", [256, 128])
    beta_t = din("beta_t", [128, 16])
    w_out = din("w_out", [D, D])
    w_gate = din("w_gate", [D, DFF])
    w_up = din("w_up", [D, DFF])
    w_down = din("w_down", [DFF, D])
    c_ident = din("c_ident", [128, 128])
    c_perm = din("c_perm", [128, 128])
    c_tri = din("c_tri", [128, 128])
    c_atri = din("c_atri", [128, 128])
    c_triU = din("c_triU", [128, 128])
    c_ones = din("c_ones", [128, 128])
    c_E = din("c_E", [64, S])
    c_maskc = din("c_maskc", [128, 17 * 128])
    c_selA = din("c_selA", [128, 32 * 64])
    c_selB = din("c_selB", [128, 32 * 64])
    c_ovl = din("c_ovl", [128, 2 * 64])
    c_freq = din("c_freq", [128, 4])

    out = nc.dram_tensor("out", [1024, D], F32, kind="ExternalOutput").ap()
    mod_src = nc.dram_tensor("mod_src", [1, 3072], F32).ap()
    mod_all = nc.dram_tensor("mod_all", [4, 3072], F32).ap()
    o_srcf = [nc.dram_tensor("o_srcf%d" % q, [1024, 256], BF16).ap() for q in range(4)]
    o_srcn = [nc.dram_tensor("o_srcn%d" % q, [1024, 256], BF16).ap() for q in range(4)]
    o_allf = nc.dram_tensor("o_allf", [4 * S, 256], BF16).ap()
    o_alln = nc.dram_tensor("o_alln", [4 * S, 256], BF16).ap()
    RG = [[0, 1, 2, 3], [4, 5, 6, 7]]
    h1_scr = nc.dram_tensor("h1_scr", [1024, D], F32).ap()

    TOTAL = 207 * 1024
    mem = Mem(nc, TOTAL)
    PS = [nc.alloc_psum_tensor("ps%d" % i, [128, 512], F32) for i in range(8)]

    def psk(b):
        return ("ps", b)

    dsem_cnt = [0]

    def dsem(name, group=False):
        dsem_cnt[0] += 1
        return sc.new_sem("d_%s_%d" % (name, dsem_cnt[0]), group)

    sem_by_q = {}

    def dma(eng, out_ap, in_ap, sem, reads=(), writes=(), slow=False):
        key = (id(sem), eng)
        if key not in sem_by_q:
            first = not any(k[0] == id(sem) for k in sem_by_q)
            sem_by_q[key] = sem if first else sc.new_sem("dq%d" % len(sem_by_q), sem.group)
        sem = sem_by_q[key]
        if slow:
            fn = lambda e: e.dma_start(out=out_ap, in_=in_ap, allow_slow_non_contiguous=True)
        else:
            fn = lambda e: e.dma_start(out=out_ap, in_=in_ap)
        sc.op(eng, fn, reads=reads, writes=writes, dsem=sem)

    ident = mem.t([128], BF16)
    perm = mem.t([128], BF16)
    tri = mem.t([128], BF16)
    atri = mem.t([128], BF16)
    identF = mem.t([128], F32)
    A1 = mem.t([16], F32)
    B1 = mem.t([16], F32)
    A2 = mem.t([16], F32)
    B2 = mem.t([16], F32)
    betaT = mem.t([16], F32)
    ostage = mem.t([NT, 256], BF16)
    small = mem.t([64], F32)
    EPSB = small[:, 32:33]
    mem_small_ss = mem.t([NT], F32)
    mem_small_rs = mem.t([NT], F32)
    P_MARK = mem.off

    s_const = dsem("const", True)
    dma("pool", ident, c_ident, s_const, writes=["ident"])
    if c_perm is not None:
        dma("pool", perm, c_perm, s_const, writes=["perm"])
    dma("pool", tri, c_tri, s_const, writes=["tri"])
    if c_atri is not None:
        dma("pool", atri, c_atri, s_const, writes=["atri"])
    dma("sp", identF, c_ident, s_const, writes=["identF"])
    dma("sp", betaT, beta_t, s_const, writes=["betaT"])
    sc.op("dve", lambda e: e.memset(EPSB, EPS), writes=["epsb"])

    ct = mem.t([16], F32)
    csil = mem.t([16], BF16)
    wa = [mem.t([16, 512], BF16) for _ in range(2)]
    modrow = mem.t([3072], F32, parts=1)
    badar = mem.t([3072], F32, parts=1)
    modrows = mem.t([128], F32)
    modT = mem.t([96], F32)
    nat = mem.t([16], F32)
    nft = mem.t([16], F32)

    s_p0 = dsem("p0", True)
    dma("sp", ct, c_t, s_p0, writes=["ct"])
    dma("sp", badar[0:1, :], b_ada, s_p0, writes=["badar"])
    dma("sp", nat, nattn_t, s_p0, writes=["nat"])
    dma("sp", nft, nffn_t, s_p0, writes=["nft"])
    sc.op("act", lambda e: e.activation(out=csil, in_=ct, func=AF.Silu), reads=["ct"], writes=["csil"])
    s_wa = [dsem("wa0"), dsem("wa1")]
    w_ada_v = w_ada.rearrange("(j p) n -> p j n", p=128)
    for nn in range(6):
        sl = nn % 2
        dma("pool", wa[sl], w_ada_v[:, :, nn * 512:(nn + 1) * 512], s_wa[sl], writes=[("wa", sl)])
        pb = PS[nn % 2]
        for j in range(16):
            sc.op("pe", (lambda e, pb=pb, sl=sl, j=j: e.matmul(pb[0:1, :], lhsT=csil[:, j:j + 1], rhs=wa[sl][:, j, :],
                                                               start=(j == 0), stop=(j == 15))),
                  reads=["csil", ("wa", sl)], writes=[psk(nn % 2)], signal=(j == 15))
        sc.op("dve", (lambda e, pb=pb, nn=nn: e.tensor_tensor(out=modrow[0:1, nn * 512:(nn + 1) * 512], in0=pb[0:1, :],
                                                              in1=badar[0:1, nn * 512:(nn + 1) * 512], op=ALU.add)),
              reads=[psk(nn % 2), "badar"], writes=["modrow"])
    s_mod = dsem("mod")
    dma("sp", mod_src, modrow[0:1, :], s_mod, reads=["modrow"], writes=["mod_src"])
    s_cc = sc.new_sem("cc0")
    sc.op("pool", lambda e: e.collective_compute("AllGather", ALU.bypass, replica_groups=[[0, 1, 2, 3], [4, 5, 6, 7]],
                                                 ins=[mod_src], outs=[mod_all]),
          reads=["mod_src"], writes=["mod_all"], dsem=s_cc, inc=1)
    mod_flat = mod_all.rearrange("r (a p) -> (r a) p", p=128)
    dma("sp", modrows[0:96, :], mod_flat, s_mod, reads=["mod_all"], writes=["modrows"])
    sc.op("pe", lambda e: e.matmul(PS[2][:, 0:96], lhsT=modrows[0:96, :], rhs=identF[0:96, 0:96], start=True, stop=True),
          reads=["modrows", "identF"], writes=[psk(2)])
    sc.op("dve", lambda e: e.tensor_copy(out=modT, in_=PS[2][:, 0:96]), reads=[psk(2)], writes=["modT"])
    sc.op("dve", lambda e: e.scalar_tensor_tensor(out=A1, in0=modT[:, 16:32], scalar=1.0, in1=nat, op0=ALU.add, op1=ALU.mult),
          reads=["modT", "nat"], writes=["A1"])
    sc.op("dve", lambda e: e.tensor_copy(out=B1, in_=modT[:, 0:16]), reads=["modT"], writes=["B1"])
    sc.op("dve", lambda e: e.scalar_tensor_tensor(out=A2, in0=modT[:, 64:80], scalar=1.0, in1=nft, op0=ALU.add, op1=ALU.mult),
          reads=["modT", "nft"], writes=["A2"])
    sc.op("dve", lambda e: e.tensor_copy(out=B2, in_=modT[:, 48:64]), reads=["modT"], writes=["B2"])
    sc.barrier()
    mem.off = P_MARK

    xs = [mem.t([D], F32) for _ in range(2)]
    xn = mem.t([D], BF16)
    utmp = mem.t([1024], F32)
    uTb = [mem.t([16, 512], BF16) for _ in range(2)]
    U_MARK = mem.off
    s_xs = [dsem("xs0"), dsem("xs1")]
    ss = small[:, 0:1]
    rstd = small[:, 1:2]
    xcount = [0]

    ssall = mem_small_ss
    rsall = mem_small_rs
    xnb = [xn, ostage[:, 8:16, :].rearrange("p a b -> p (a b)")]
    sqjunk = ostage[:, 0:8, :].rearrange("p a b -> p (a b)")

    def tile_dma(n):
        sl = n % 2
        dma("sp", xs[sl], x_b[n * 128:(n + 1) * 128, :], s_xs[sl], writes=[("xs", sl)])

    def tile_act(n):
        sl = n % 2
        xb = xnb[sl]
        ssc = ssall[:, n:n + 1]
        rsc = rsall[:, n:n + 1]
        sc.op("act", lambda e: e.activation(out=sqjunk, in_=xs[sl], func=AF.Square, accum_out=ssc),
              reads=[("xs", sl), "ssall"], writes=[("ss", n)])
        sc.op("act", lambda e: e.activation(out=rsc, in_=ssc, func=AF.Ln, scale=1.0 / D, bias=EPSB), reads=[("ss", n), "epsb"], writes=[("rs", n)])
        sc.op("act", lambda e: e.activation(out=rsc, in_=rsc, func=AF.Exp, scale=-0.5), reads=[("rs", n)], writes=[("rs", n)])
        sc.op("act", lambda e: e.activation(out=xb, in_=xs[sl], func=AF.Copy, scale=rsc),
              reads=[("xs", sl), ("rs", n)], writes=[("xn", sl)])

    def tile_pe(n):
        sl = n % 2
        xb = xnb[sl]
        ub = (n // 4) % 2
        tl = n % 4
        for half in range(2):
            pb = PS[half]
            pbv = pb[:, :].bitcast(BF16)
            for jj in range(8):
                j = half * 8 + jj
                sc.op("pe", (lambda e, pbv=pbv, jj=jj, j=j: e.transpose(pbv[:, jj * 128:(jj + 1) * 128],
                                                                        xb[:, j * 128:(j + 1) * 128], ident)),
                      reads=[("xn", sl), "ident"], writes=[psk(half)], signal=(jj == 7))
            pv3 = pbv.rearrange("p (a b) -> p a b", a=8)
            a1b = A1[:, half * 8:(half + 1) * 8].unsqueeze(2).to_broadcast([128, 8, 128])
            b1b = B1[:, half * 8:(half + 1) * 8].unsqueeze(2).to_broadcast([128, 8, 128])
            ut3 = utmp.rearrange("p (a b) -> p a b", a=8)
            sc.op("dve", (lambda e, ut3=ut3, pv3=pv3, a1b=a1b: e.tensor_tensor(out=ut3, in0=pv3, in1=a1b, op=ALU.mult)),
                  reads=[psk(half), "A1"], writes=["utmp"])
            sc.op("dve", (lambda e, ut3=ut3, b1b=b1b, half=half: e.tensor_tensor(
                out=uTb[ub][:, half * 8:(half + 1) * 8, tl * 128:(tl + 1) * 128], in0=ut3, in1=b1b, op=ALU.add)),
                reads=["utmp", "B1"], writes=[("uT", ub)])

    def run_projection(groups_of, nst=8, after_prologue=None):
        ntile = nst * 4
        sc.op("pool", lambda e: e.memset(ssall, 0.0), writes=["ssall"])

        def slot(n):
            if n + 2 < ntile:
                tile_dma(n + 2)
            if n + 1 < ntile:
                tile_act(n + 1)
            tile_pe(n)
        tile_dma(0)
        tile_dma(1)
        tile_act(0)
        for n in range(4):
            slot(n)
        if after_prologue is not None:
            after_prologue()
        for st in range(nst):
            groups = groups_of(st)
            ng = len(groups)
            for k in range(4):
                n = 4 * (st + 1) + k
                if n < ntile:
                    slot(n)
                for g in groups[(k * ng) // 4:((k + 1) * ng) // 4]:
                    g()

    def proj_fm(W, col0, dst_fn, st, bank, post=None):
        pb = PS[bank]
        uT = uTb[st % 2]
        for j in range(16):
            sc.op("pe", (lambda e, j=j: e.matmul(pb[:, :], lhsT=W[:, j, col0:col0 + 128], rhs=uT[:, j, :],
                                                 start=(j == 0), stop=(j == 15))),
                  reads=["W", ("uT", st % 2)], writes=[psk(bank)], signal=(j == 15))
        dst_fn(pb)

    if stage >= 0.1:
        Wf = mem.t([16, NFOX], BF16)
        WN_BYTES = (16 * NNSA * 2 + 63) // 64 * 64
        mem.off = max(mem.off, U_MARK + WN_BYTES)
        qT = mem.t([2, S], BF16)
        kT = mem.t([2, S], BF16)
        Vf = mem.t([NT, 2, 130], BF16)
        nlf = mem.t([NT, 2], F32)
        bfg = mem.t([2], F32)
        triU = mem.t([128], F32)
        onesF = mem.t([128], F32)
        ncw = mem.t([NT, 2], F32)
        tot = mem.t([NT, 2], F32)
        offs = mem.t([NT, 2], F32)
        ncum = mem.t([NT, 2], F32)
        ncref = mem.t([NT, 2], F32)
        Tb = mem.t([2, NT, NT], F32)
        pT = mem.t([8, 128], BF16)
        ftmp = mem.t([8], F32)
        s_w = dsem("wf", True)
        dma("pool", Wf, w_fox.rearrange("(j p) n -> p j n", p=128), s_w, writes=["W"])
        dma("sp", bfg, b_fg[0].partition_broadcast(128), s_w, writes=["bfg"])
        dma("sp", triU, c_triU, s_w, writes=["triU"])
        dma("sp", onesF, c_ones, s_w, writes=["onesF"])
        sc.op("pool", lambda e: e.memset(Vf[:, :, :, 128:129], 1.0), writes=["Vf"])
        sc.op("pool", lambda e: e.memset(Vf[:, :, :, 129:130], 0.0), writes=["Vf"])
        def fox_groups(st):
            groups = []
            uT = uTb[st % 2]
            for cc in range(4):
                bank = 2 + cc % 2
                dstT = qT if cc < 2 else kT
                hh = cc % 2

                def g_fm(cc=cc, bank=bank, dstT=dstT, hh=hh):
                    def ev(pb):
                        sc.op("act", lambda e: e.copy(out=dstT[:, hh, st * 512:(st + 1) * 512], in_=pb[:, :]),
                              reads=[psk(bank)], writes=["qkT"])
                    proj_fm(Wf, cc * 128, ev, st, bank)
                groups.append(g_fm)
            for tl in range(4):
                def g_tm(tl=tl):
                    t = st * 4 + tl
                    bank = 4 + tl % 2
                    pb = PS[bank]
                    for j in range(16):
                        sc.op("pe", (lambda e, j=j: e.matmul(pb[:, 0:258], lhsT=uT[:, j, tl * 128:(tl + 1) * 128],
                                                             rhs=Wf[:, j, 512:770], start=(j == 0), stop=(j == 15))),
                              reads=["W", ("uT", st % 2)], writes=[psk(bank)], signal=(j == 15))
                    sc.op("act", lambda e: e.copy(out=Vf[:, t, :, 0:128], in_=pb[:, 0:256].rearrange("p (h d) -> p h d", h=2)),
                          reads=[psk(bank)], writes=["Vf"])
                    f0 = ftmp[:, 2 * (tl % 2):2 * (tl % 2) + 2]
                    fk = ("f0", tl % 2)
                    sc.op("dve", lambda e: e.tensor_tensor(out=f0, in0=pb[:, 256:258], in1=bfg, op=ALU.add),
                          reads=[psk(bank), "bfg"], writes=[fk])
                    sc.op("act", lambda e: e.activation(out=f0, in_=f0, func=AF.Exp, scale=-1.0), reads=[fk], writes=[fk])
                    sc.op("act", lambda e: e.activation(out=nlf[:, t, :], in_=f0, func=AF.Ln, bias=1.0), reads=[fk], writes=["nlf"])
                groups.append(g_tm)
            return groups
        run_projection(fox_groups)
        Wn_pref = None
        if stage >= 2:
            Wn_pref = mem.at(U_MARK, [16, NNSA], BF16)
            s_wn = dsem("wnpre")
            dma("pool", Wn_pref, w_nsa.rearrange("(j p) n -> p j n", p=128), s_wn, writes=["W"])
        def fox_cum():
            nlf2 = nlf.rearrange("p a b -> p (a b)")
            sc.op("pe", lambda e: e.matmul(PS[0][:, 0:64], lhsT=triU, rhs=nlf2, start=True, stop=True),
                  reads=["triU", "nlf"], writes=[psk(0)])
            sc.op("pe", lambda e: e.matmul(PS[1][:, 0:64], lhsT=onesF, rhs=nlf2, start=True, stop=True),
                  reads=["onesF", "nlf"], writes=[psk(1)])
            sc.op("dve", lambda e: e.tensor_copy(out=ncw.rearrange("p a b -> p (a b)"), in_=PS[0][:, 0:64]),
                  reads=[psk(0)], writes=["ncw"])
            sc.op("dve", lambda e: e.tensor_copy(out=tot.rearrange("p a b -> p (a b)"), in_=PS[1][:, 0:64]),
                  reads=[psk(1)], writes=["tot"])
            sc.op("dve", lambda e: e.memset(offs[:, 0, :], 0.0), writes=["offs"])
            for c in range(1, NT):
                sc.op("dve", (lambda e, c=c: e.tensor_tensor(out=offs[:, c, :], in0=offs[:, c - 1, :], in1=tot[:, c - 1, :], op=ALU.add)),
                      reads=["offs", "tot"], writes=["offs"])
            sc.op("dve", lambda e: e.tensor_tensor(out=ncum, in0=ncw, in1=offs, op=ALU.add), reads=["ncw", "offs"], writes=["ncum"])
            sc.op("dve", lambda e: e.scalar_tensor_tensor(out=ncref, in0=tot, scalar=0.5, in1=offs, op0=ALU.mult, op1=ALU.add),
                  reads=["tot", "offs"], writes=["ncref"])
            for h in range(2):
                for i in range(NT):
                    sc.op("dve", (lambda e, h=h, i=i: e.tensor_scalar(out=Tb[:, h, i, 0:i + 1], in0=ncum[:, 0:i + 1, h],
                                                                      scalar1=ncref[:, i, h:h + 1], scalar2=None, op0=ALU.subtract)),
                          reads=["ncum", "ncref"], writes=["Tb"])

        def fox_attn():
            pcount = [0]
            steps = [(h, I, j) for h in range(2) for I in range(8) for j in range(4 * I + 4)]
            LOOK, NB = 3, 4

            def qk(n):
                h, I, j = steps[n]
                sb = n % NB
                sc.op("pe", (lambda e, sb=sb, h=h, j=j, I=I: e.matmul(PS[sb][:, :], lhsT=kT[:, h, j * 128:(j + 1) * 128],
                                                                      rhs=qT[:, h, I * 512:(I + 1) * 512], start=True, stop=True)),
                      reads=["qkT"], writes=[psk(sb)])
            for n in range(min(LOOK, len(steps))):
                qk(n)
            for n, (h, I, j) in enumerate(steps):
                if n + LOOK < len(steps):
                    qk(n + LOOK)
                sb = n % NB
                for iq in range(4):
                    i = 4 * I + iq
                    if j > i:
                        continue
                    slot = pcount[0] % 8
                    pcount[0] += 1
                    sc.op("act", (lambda e, sb=sb, slot=slot, iq=iq, h=h, i=i, j=j: e.activation(
                        out=pT[:, slot, :], in_=PS[sb][:, iq * 128:(iq + 1) * 128], func=AF.Exp,
                        bias=Tb[:, h, i, j:j + 1], scale=SCALE)),
                        reads=[psk(sb), "Tb"], writes=[("pT", slot)])
                    if j == i:
                        sc.op("dve", (lambda e, slot=slot: e.tensor_tensor(out=pT[:, slot, :], in0=pT[:, slot, :], in1=tri, op=ALU.mult)),
                              reads=[("pT", slot), "tri"], writes=[("pT", slot)])
                    sc.op("pe", (lambda e, slot=slot, iq=iq, h=h, j=j, i=i: e.matmul(
                        PS[4 + iq][:, 0:130], lhsT=pT[:, slot, :], rhs=Vf[:, j, h, :], start=(j == 0), stop=(j == i))),
                        reads=[("pT", slot), "Vf"], writes=[psk(4 + iq)], signal=(j == i))
                    if j == i:
                        rs = small[:, 8 + iq:9 + iq]
                        sc.op("dve", (lambda e, iq=iq, rs=rs: e.reciprocal(out=rs, in_=PS[4 + iq][:, 128:129])),
                              reads=[psk(4 + iq)], writes=[("rs", iq)])
                        sc.op("dve", (lambda e, iq=iq, rs=rs, i=i, h=h: e.tensor_scalar(
                            out=ostage[:, i, h * 128:(h + 1) * 128], in0=PS[4 + iq][:, 0:128], scalar1=rs, scalar2=None, op0=ALU.mult)),
                            reads=[psk(4 + iq), ("rs", iq)], writes=["ostage"])
        if stage >= 0.6:
            fox_cum()
        if stage >= 1:
            fox_attn()
        s_of = dsem("osrc_f", True)
        for q in range(4):
            dma("sp", o_srcf[q].rearrange("(t p) f -> p t f", p=128), ostage[:, 8 * q:8 * q + 8, :], s_of,
                reads=["ostage"], writes=["o_srcf"])
        sc.barrier()
        mem.off = U_MARK

    if stage >= 2:
        Wn = mem.t([16, NNSA], BF16)
        W_MARK = mem.off
        qn = mem.t([4, S], BF16)
        ksT = mem.t([S], BF16)
        kwT = mem.t([S], BF16)
        kcr = mem.t([S], BF16)
        vcr = mem.t([S], BF16)
        Vsw = mem.t([NT, 2, 130], BF16)
        gat = mem.t([NT, 6], F32)
        posf = mem.t([512], F32)
        cosT = mem.t([512], F32)
        sinT = mem.t([512], F32)
        rtmp = mem.t([512], BF16)
        rt1 = mem.t([512], F32)
        rt2 = mem.t([512], F32)
        posi = rt2.bitcast(I32)
        kint = posf.bitcast(I32)
        frq = mem.t([4], F32)
        s_w = dsem("wn", True)
        def fox_exchange():
            if stage < 3:
                return
            s_ccf = sc.new_sem("ccf", True)
            for q in range(4):
                sc.op("pool", (lambda e, q=q: e.collective_compute("AllGather", ALU.bypass, replica_groups=RG,
                                                                   ins=[o_srcf[q]], outs=[o_allf[q * S:(q + 1) * S, :]])),
                      reads=[], writes=[("o_allf", q)], dsem=s_ccf, inc=1)
        dma("sp", frq, c_freq, s_w, writes=["frq"])
        sc.op("pool", lambda e: e.memset(Vsw[:, :, :, 128:129], 1.0), writes=["Vsw"])
        sc.op("pool", lambda e: e.memset(Vsw[:, :, :, 129:130], 0.0), writes=["Vsw"])
        s_pos = dsem("pos")

        def rope_tables(src_ap, n, tag):
            dma("sp", posi[:, 0:n], src_ap.partition_broadcast(128), s_pos, writes=["rt2"])
            sc.op("dve", lambda e: e.tensor_copy(out=posf[:, 0:n], in_=posi[:, 0:n]), reads=["rt2"], writes=["posf"])
            sc.op("dve", lambda e: e.tensor_scalar(out=rt1[:, 0:n], in0=posf[:, 0:n], scalar1=frq[:, 0:1], scalar2=None, op0=ALU.mult),
                  reads=["posf", "frq"], writes=["rt1"])
            for dstT, addc, scl in ((sinT, 0.0, frq[:, 1:2]), (cosT, float(np.pi / 2), 1.0)):
                sc.op("dve", (lambda e, addc=addc: e.tensor_scalar(out=rt2[:, 0:n], in0=rt1[:, 0:n], scalar1=addc, scalar2=float(1.0 / TWO_PI),
                                                                   op0=ALU.add, op1=ALU.mult)), reads=["rt1"], writes=["rt2"])
                sc.op("dve", lambda e: e.tensor_copy(out=kint[:, 0:n], in_=rt2[:, 0:n]), reads=["rt2"], writes=["posf"])
                sc.op("dve", lambda e: e.tensor_copy(out=rt2[:, 0:n], in_=kint[:, 0:n]), reads=["posf"], writes=["rt2"])
                sc.op("dve", lambda e: e.scalar_tensor_tensor(out=rt2[:, 0:n], in0=rt2[:, 0:n], scalar=float(-TWO_PI), in1=rt1[:, 0:n],
                                                              op0=ALU.mult, op1=ALU.add), reads=["rt2", "rt1"], writes=["rt2"])
                sc.op("dve", (lambda e, addc=addc: e.tensor_scalar(out=rt2[:, 0:n], in0=rt2[:, 0:n], scalar1=addc, scalar2=float(np.pi),
                                                                   op0=ALU.add, op1=ALU.min)), reads=["rt2"], writes=["rt2"])
                sc.op("dve", lambda e: e.tensor_scalar(out=rt2[:, 0:n], in0=rt2[:, 0:n], scalar1=float(-np.pi), scalar2=None, op0=ALU.max),
                      reads=["rt2"], writes=["rt2"])
                sc.op("act", (lambda e, dstT=dstT, scl=scl: e.activation(out=dstT[:, 0:n], in_=rt2[:, 0:n], func=AF.Sin, scale=scl)),
                      reads=["rt2", "frq"], writes=["sinT" if dstT is sinT else "cosT"])

        def rope_evac(pb, bank, dst, n):
            sc.op("act", lambda e: e.copy(out=rtmp[:, 0:n], in_=pb[:, 0:n]), reads=[psk(bank)], writes=["rtmp"])
            sc.op("pe", lambda e: e.matmul(PS[7][:, 0:n], lhsT=perm, rhs=rtmp[:, 0:n], start=True, stop=True),
                  reads=["rtmp", "perm"], writes=[psk(7)])
            sc.op("dve", lambda e: e.tensor_tensor(out=rt1[:, 0:n], in0=pb[:, 0:n], in1=cosT[:, 0:n], op=ALU.mult),
                  reads=[psk(bank), "cosT"], writes=["rt1"])
            sc.op("dve", lambda e: e.tensor_tensor(out=rt2[:, 0:n], in0=PS[7][:, 0:n], in1=sinT[:, 0:n], op=ALU.mult),
                  reads=[psk(7), "sinT"], writes=["rt2"])
            sc.op("dve", lambda e: e.tensor_tensor(out=dst, in0=rt1[:, 0:n], in1=rt2[:, 0:n], op=ALU.add),
                  reads=["rt1", "rt2"], writes=["nsaT"])

        def nsa_groups(st):
            groups = []
            uT = uTb[st % 2]
            cols = slice(st * 512, (st + 1) * 512)
            groups.append(lambda: rope_tables(pos_b[0:1, st * 512:(st + 1) * 512][0], 512, "t"))

            def g_fm(cc):
                bank = 2 + cc % 2
                if cc < 6:
                    dst = qn[:, cc, cols] if cc < 4 else (ksT[:, cols] if cc == 4 else kwT[:, cols])
                    proj_fm(Wn, cc * 128, (lambda pb: rope_evac(pb, bank, dst, 512)), st, bank)
                else:
                    dst = kcr[:, cols] if cc == 6 else vcr[:, cols]

                    def ev(pb):
                        sc.op("act", lambda e: e.copy(out=dst, in_=pb[:, :]), reads=[psk(bank)], writes=["nsaT"])
                    proj_fm(Wn, cc * 128, ev, st, bank)

            def g_tm(tl):
                t = st * 4 + tl
                bank = 4 + tl % 2
                pb = PS[bank]
                for j in range(16):
                    sc.op("pe", (lambda e, j=j: e.matmul(pb[:, 0:NNSA_TM], lhsT=uT[:, j, tl * 128:(tl + 1) * 128],
                                                         rhs=Wn[:, j, NNSA_FM:NNSA], start=(j == 0), stop=(j == 15))),
                          reads=["W", ("uT", st % 2)], writes=[psk(bank)], signal=(j == 15))
                sc.op("act", lambda e: e.copy(out=Vsw[:, t, :, 0:128], in_=pb[:, 0:256].rearrange("p (h d) -> p h d", h=2)),
                      reads=[psk(bank)], writes=["Vsw"])
                gk = ("gat", t)
                sc.op("act", lambda e: e.activation(out=gat[:, t, :], in_=pb[:, 256:262], func=AF.Exp, scale=-1.0),
                      reads=[psk(bank)], writes=[gk])
                sc.op("dve", lambda e: e.tensor_scalar(out=gat[:, t, :], in0=gat[:, t, :], scalar1=1.0, scalar2=None, op0=ALU.add),
                      reads=[gk], writes=[gk])
                sc.op("dve", lambda e: e.reciprocal(out=gat[:, t, :], in_=gat[:, t, :]), reads=[gk], writes=[gk, "gat"])
            for cc in (6, 7):
                groups.append(lambda cc=cc: g_fm(cc))
            for tl in range(4):
                groups.append(lambda tl=tl: g_tm(tl))
            for cc in range(6):
                groups.append(lambda cc=cc: g_fm(cc))
            return groups
        run_projection(nsa_groups, after_prologue=fox_exchange)
        sc.barrier()
        mem.off = P_MARK
        KcT = mem.t([256], BF16)
        Vca = mem.t([2, 193], BF16)
        C_MARK = mem.off
        cpT = mem.t([32], F32)
        kcp = mem.t([32, 256], BF16)
        w1 = mem.t([32, 256], BF16)
        w2 = mem.t([2, 128], BF16)
        hT = mem.t([2, 256], F32)
        hg = mem.t([2, 256], BF16)
        gt1 = mem.t([256], F32)
        gt2 = mem.t([256], F32)
        assert mem.off <= W_MARK, (mem.off, W_MARK)
        s_c = dsem("cmp", True)
        dma("sp", cpT, cposT, s_c, writes=["cpT"])
        dma("pool", Vca[:, :, 128:192], c_ovl.rearrange("p (a b) -> p a b", a=2), s_c, writes=["Vca"])
        sc.op("pool", lambda e: e.memset(Vca[:, :, 192:193], 1.0), writes=["Vca"])
        sc.op("pool", lambda e: e.memset(kcp, 0.0), writes=["kcp"])
        rope_tables(pos_end[0], 256, "e")
        s_w1 = dsem("w1", True)
        for which in range(2):
            src = kcr if which == 0 else vcr
            dma("pool", w1, (w_kc1 if which == 0 else w_vc1).rearrange("(l p) n -> p l n", p=128), s_w1, writes=["w1"])
            dma("pool", w2, (w_kc2 if which == 0 else w_vc2).rearrange("(c p) n -> p c n", p=128), s_w1, writes=["w2"])
            srcv = src.rearrange("p (c l) -> p l c", l=16)
            for l in range(32):
                lo, sh = l % 16, l // 16
                sc.op("dve", (lambda e, l=l, lo=lo, sh=sh, srcv=srcv: e.tensor_scalar(
                    out=kcp[:, l, 0:255], in0=srcv[:, lo, sh:sh + 255], scalar1=cpT[:, l:l + 1], scalar2=None, op0=ALU.add)),
                    reads=["nsaT", "cpT"], writes=["kcp"])
            for ncn in range(2):
                for l in range(32):
                    sc.op("pe", (lambda e, ncn=ncn, l=l: e.matmul(PS[ncn][:, 0:256], lhsT=w1[:, l, ncn * 128:(ncn + 1) * 128], rhs=kcp[:, l, :],
                                                                  start=(l == 0), stop=(l == 31))),
                          reads=["w1", "kcp"], writes=[psk(ncn)], signal=(l == 31))
                sc.op("act", (lambda e, ncn=ncn: e.copy(out=hT[:, ncn, :], in_=PS[ncn][:, 0:256])), reads=[psk(ncn)], writes=["hT"])
                sc.op("dve", (lambda e, ncn=ncn: e.tensor_tensor(out=gt1, in0=hT[:, ncn, :], in1=hT[:, ncn, :], op=ALU.mult)),
                      reads=["hT"], writes=["gt1"])
                sc.op("dve", lambda e: e.tensor_scalar(out=gt1, in0=gt1, scalar1=0.044715, scalar2=1.0, op0=ALU.mult, op1=ALU.add),
                      reads=["gt1"], writes=["gt1"])
                sc.op("dve", (lambda e, ncn=ncn: e.tensor_tensor(out=gt1, in0=gt1, in1=hT[:, ncn, :], op=ALU.mult)),
                      reads=["gt1", "hT"], writes=["gt1"])
                sc.op("act", lambda e: e.activation(out=gt2, in_=gt1, func=AF.Sigmoid, scale=1.5957691216), reads=["gt1"], writes=["gt2"])
                sc.op("dve", (lambda e, ncn=ncn: e.tensor_tensor(out=hg[:, ncn, :], in0=gt2, in1=hT[:, ncn, :], op=ALU.mult)),
                      reads=["gt2", "hT"], writes=["hg"])
            if which == 0:
                for ncn in range(2):
                    sc.op("pe", (lambda e, ncn=ncn: e.matmul(PS[2][:, 0:256], lhsT=w2[:, ncn, :], rhs=hg[:, ncn, :],
                                                             start=(ncn == 0), stop=(ncn == 1))),
                          reads=["w2", "hg"], writes=[psk(2)], signal=(ncn == 1))
                rope_evac(PS[2], 2, KcT[:, 0:256], 256)
                sc.op("dve", lambda e: e.memset(KcT[:, 255:256], 0.0), reads=["nsaT"], writes=["nsaT"])
            else:
                for ch in range(2):
                    for ncn in range(2):
                        sc.op("pe", (lambda e, ncn=ncn, ch=ch: e.matmul(PS[3][:, 0:128], lhsT=hg[:, ncn, ch * 128:(ch + 1) * 128], rhs=w2[:, ncn, :],
                                                                        start=(ncn == 0), stop=(ncn == 1))),
                              reads=["w2", "hg"], writes=[psk(3)], signal=(ncn == 1))
                    sc.op("act", (lambda e, ch=ch: e.copy(out=Vca[:, ch, 0:128], in_=PS[3][:, 0:128])), reads=[psk(3)], writes=["Vca"])
        sc.barrier()
        mem.off = C_MARK
        EE = mem.t([S], BF16)
        mkc = mem.t([17, 128], BF16)
        sA = mem.t([32, 64], F32)
        sB = mem.t([32, 64], F32)
        pT = mem.t([8, 128], BF16)
        pc = mem.t([2, 512], BF16)
        oc = [mem.t([4, 2, 128], F32) for _ in range(2)]
        imp = mem.t([64], F32)
        scr = mem.t([64], F32)
        wk = mem.t([64], F32)
        m8 = mem.t([8], F32)
        m8b = mem.t([8], F32)
        nsel = mem.t([64], BF16)
        nselT = [mem.t([512], BF16) for _ in range(2)]
        cf = mem.t([16], F32)
        assert mem.off <= W_MARK, (mem.off, W_MARK)
        s_a = dsem("att", True)
        dma("pool", EE[0:64, :], c_E, s_a, writes=["EE"])
        dma("pool", mkc, c_maskc.rearrange("p (a b) -> p a b", a=17), s_a, writes=["mkc"])
        dma("sp", sA, c_selA.rearrange("p (a b) -> p a b", a=32), s_a, writes=["sA"])
        dma("sp", sB, c_selB.rearrange("p (a b) -> p a b", a=32), s_a, writes=["sB"])
        pcount = [0]
        scount = [0]

        def recip_col(dst, src_ap, bank, key):
            sc.op("dve", lambda e: e.tensor_scalar(out=dst, in0=src_ap, scalar1=1e-30, scalar2=None, op0=ALU.max),
                  reads=[psk(bank)], writes=[key])
            sc.op("dve", lambda e: e.reciprocal(out=dst, in_=dst), reads=[key], writes=[key])

        def comp_gen(I):
            par = I % 2
            for iq in range(4):
                i = 4 * I + iq
                chunks = [0] if i < 16 else [0, 1]
                qv = qn[:, :, i * 128:(i + 1) * 128]
                for ch in chunks:
                    sc.op("pe", (lambda e, ch=ch, qv=qv: e.matmul(PS[2][:, :].rearrange("p (h q) -> p h q", h=4),
                                                                  lhsT=KcT[:, ch * 128:(ch + 1) * 128], rhs=qv, start=True, stop=True)),
                          reads=["KcT", "nsaT"], writes=[psk(2)])
                    sc.op("act", (lambda e, ch=ch: e.activation(out=pc[:, ch, :], in_=PS[2][:, :], func=AF.Exp, scale=SCALE)),
                          reads=[psk(2)], writes=[("pc", ch)])
                    ip = i - 16 * ch
                    if ip <= 16:
                        mb = mkc[:, ip, :].unsqueeze(1).to_broadcast([128, 4, 128])
                        sc.op("dve", (lambda e, ch=ch, mb=mb: e.tensor_tensor(out=pc[:, ch, :].rearrange("p (h q) -> p h q", h=4),
                                                                              in0=pc[:, ch, :].rearrange("p (h q) -> p h q", h=4),
                                                                              in1=mb, op=ALU.mult)),
                              reads=[("pc", ch), "mkc"], writes=[("pc", ch)])
                    yield
                for rp in range(2):
                    for r in (2 * rp, 2 * rp + 1):
                        for ci, ch in enumerate(chunks):
                            sc.op("pe", (lambda e, r=r, ch=ch, ci=ci, chunks=chunks: e.matmul(
                                PS[3][:, (r % 2) * 193:(r % 2) * 193 + 193], lhsT=pc[:, ch, r * 128:(r + 1) * 128], rhs=Vca[:, ch, :],
                                start=(ci == 0), stop=(ci == len(chunks) - 1))),
                                reads=[("pc", ch), "Vca"], writes=[psk(3)], signal=(ci == len(chunks) - 1))
                    yield
                    for r in (2 * rp, 2 * rp + 1):
                        bank = 3
                        base = (r % 2) * 193
                        rc = cf[:, r:r + 1]
                        recip_col(rc, PS[bank][:, base + 192:base + 193], bank, ("cf", r))
                        if r == 0:
                            sc.op("dve", (lambda e, bank=bank, base=base, rc=rc: e.tensor_scalar(out=imp, in0=PS[bank][:, base + 128:base + 192],
                                                                                                 scalar1=rc, scalar2=None, op0=ALU.mult)),
                                  reads=[psk(bank), ("cf", r)], writes=["imp"])
                        else:
                            sc.op("dve", (lambda e, bank=bank, base=base, rc=rc: e.scalar_tensor_tensor(out=imp, in0=PS[bank][:, base + 128:base + 192],
                                                                                                        scalar=rc, in1=imp, op0=ALU.mult, op1=ALU.add)),
                                  reads=[psk(bank), ("cf", r), "imp"], writes=["imp"])
                        if r < 2:
                            gc = cf[:, 8 + r:9 + r]
                            sc.op("dve", (lambda e, gc=gc, rc=rc, i=i, r=r: e.tensor_tensor(out=gc, in0=rc, in1=gat[:, i, 3 * r:3 * r + 1], op=ALU.mult)),
                                  reads=[("cf", r), "gat"], writes=[("cf", 8 + r)])
                            sc.op("dve", (lambda e, bank=bank, base=base, gc=gc, iq=iq, r=r, par=par: e.tensor_scalar(
                                out=oc[par][:, iq, r, :], in0=PS[bank][:, base:base + 128], scalar1=gc, scalar2=None, op0=ALU.mult)),
                                reads=[psk(bank), ("cf", 8 + r)], writes=[("oc", par, iq, r)])
                sc.op("dve", (lambda e, i=i: e.tensor_tensor(out=scr, in0=imp, in1=sA[:, i, :], op=ALU.mult)), reads=["imp", "sA"], writes=["scr"])
                sc.op("dve", (lambda e, i=i: e.tensor_tensor(out=scr, in0=scr, in1=sB[:, i, :], op=ALU.add)), reads=["scr", "sB"], writes=["scr"])
                sc.op("dve", lambda e: e.max(out=m8, in_=scr), reads=["scr"], writes=["m8"])
                sc.op("dve", lambda e: e.match_replace(out=wk, in_to_replace=m8, in_values=scr, imm_value=-3.0e38), reads=["m8", "scr"], writes=["wk"])
                sc.op("dve", lambda e: e.max(out=m8b, in_=wk), reads=["wk"], writes=["m8b"])
                sc.op("dve", lambda e: e.tensor_scalar(out=wk, in0=scr, scalar1=m8b[:, 7:8], scalar2=None, op0=ALU.is_ge),
                      reads=["scr", "m8b"], writes=["wk"])
                sc.op("dve", lambda e: e.tensor_scalar(out=nsel, in0=wk, scalar1=-1.0, scalar2=BIG, op0=ALU.add, op1=ALU.mult),
                      reads=["wk"], writes=["nsel"])
                yield
                pbv = PS[2][:, :].bitcast(BF16)
                sc.op("pe", (lambda e, pbv=pbv: e.transpose(pbv[0:64, 0:128], nsel, ident)), reads=["nsel", "ident"], writes=[psk(2)])
                sc.op("act", (lambda e, pbv=pbv, iq=iq, par=par: e.copy(out=nselT[par][0:64, iq * 128:(iq + 1) * 128], in_=pbv[0:64, 0:128])),
                      reads=[psk(2)], writes=[("nselT", par)])
                yield

        def selwin(I, gen):
            par = I % 2
            steps = []
            for br in range(2):
                jlo = 0 if br == 0 else max(0, 4 * I - 4)
                for m in range(2):
                    for j in range(jlo, 4 * I + 4):
                        steps.append((br, m, j, jlo))

            def qk(n):
                br, m, j, jlo = steps[n]
                sb = n % 2
                kT_ = ksT if br == 0 else kwT
                sc.op("pe", (lambda e, sb=sb, m=m, j=j, kT_=kT_, br=br, I=I: e.matmul(PS[sb][:, :], lhsT=kT_[:, j * 128:(j + 1) * 128],
                                                                                 rhs=qn[:, m, I * 512:(I + 1) * 512], start=True, stop=(br == 1))),
                      reads=["nsaT"], writes=[psk(sb)], signal=(br == 1))
                if br == 0:
                    sc.op("pe", (lambda e, sb=sb, j=j, par=par: e.matmul(PS[sb][:, :], lhsT=EE[0:64, j * 128:(j + 1) * 128], rhs=nselT[par][0:64, :],
                                                                         start=False, stop=True)),
                          reads=["EE", ("nselT", par)], writes=[psk(sb)])
            qk(0)
            for n, (br, m, j, jlo) in enumerate(steps):
                if n + 1 < len(steps):
                    qk(n + 1)
                sb = n % 2
                for iq in range(4):
                    i = 4 * I + iq
                    if j > i or (br == 1 and j < i - 4):
                        continue
                    slot = pcount[0] % 8
                    pcount[0] += 1
                    sc.op("act", (lambda e, sb=sb, slot=slot, iq=iq: e.activation(out=pT[:, slot, :], in_=PS[sb][:, iq * 128:(iq + 1) * 128],
                                                                                  func=AF.Exp, scale=SCALE)),
                          reads=[psk(sb)], writes=[("pT", slot)])
                    msk = tri if j == i else (atri if (br == 1 and j == i - 4) else None)
                    if msk is not None:
                        sc.op("dve", (lambda e, slot=slot, msk=msk: e.tensor_tensor(out=pT[:, slot, :], in0=pT[:, slot, :], in1=msk, op=ALU.mult)),
                              reads=[("pT", slot), "tri", "atri"], writes=[("pT", slot)])
                    first = (j == jlo) if br == 0 else (j == max(0, i - 4))
                    sc.op("pe", (lambda e, slot=slot, iq=iq, j=j, br=br, first=first, i=i: e.matmul(
                        PS[4 + iq][:, 0:130], lhsT=pT[:, slot, :], rhs=Vsw[:, j, br, :], start=first, stop=(j == i))),
                        reads=[("pT", slot), "Vsw"], writes=[psk(4 + iq)], signal=(j == i))
                    if j == i:
                        rc = cf[:, 4 + iq:5 + iq]
                        recip_col(rc, PS[4 + iq][:, 128:129], 4 + iq, ("cf", 4 + iq))
                        sc.op("dve", (lambda e, rc=rc, i=i, m=m, br=br: e.tensor_tensor(out=rc, in0=rc, in1=gat[:, i, 3 * m + 1 + br:3 * m + 2 + br], op=ALU.mult)),
                              reads=[("cf", 4 + iq), "gat"], writes=[("cf", 4 + iq)])
                        sc.op("dve", (lambda e, rc=rc, iq=iq, m=m, par=par: e.scalar_tensor_tensor(out=oc[par][:, iq, m, :], in0=PS[4 + iq][:, 0:128], scalar=rc,
                                                                                                   in1=oc[par][:, iq, m, :], op0=ALU.mult, op1=ALU.add)),
                              reads=[psk(4 + iq), ("cf", 4 + iq), ("oc", par, iq, m)], writes=[("oc", par, iq, m)])
                        if br == 1:
                            sc.op("act", (lambda e, iq=iq, m=m, i=i, par=par: e.copy(out=ostage[:, i, m * 128:(m + 1) * 128], in_=oc[par][:, iq, m, :])),
                                  reads=[("oc", par, iq, m)], writes=[("ostage", i // 8)])
                if gen is not None:
                    next(gen, None)
            if gen is not None:
                for _ in gen:
                    pass

        for _ in comp_gen(0):
            pass
        s_on = dsem("osrc_n", True)
        s_cc2 = sc.new_sem("cc1", True)
        for I in range(8):
            selwin(I, comp_gen(I + 1) if I < 7 else None)
            if I % 2 == 1:
                q = I // 2
                dma("sp", o_srcn[q].rearrange("(t p) f -> p t f", p=128), ostage[:, 8 * q:8 * q + 8, :], s_on,
                    reads=[("ostage", q)], writes=[("o_srcn", q)])
                if stage >= 3:
                    sc.op("pool", (lambda e, q=q: e.collective_compute("AllGather", ALU.bypass, replica_groups=RG,
                                                                       ins=[o_srcn[q]], outs=[o_alln[q * S:(q + 1) * S, :]])),
                          reads=[("o_srcn", q)], writes=[("o_alln", q)], dsem=s_cc2, inc=1)
        sc.barrier()
    mem.off = P_MARK

    if stage < 3:
        dbg = nc.dram_tensor("dbg_o", [S, 512], BF16, kind="ExternalOutput").ap()
        dbgm = nc.dram_tensor("dbg_mod", [4, 3072], F32, kind="ExternalOutput").ap()
        mem.off = P_MARK
        stg = mem.t([3072], F32)
        s_d = dsem("dbg", True)
        ncol = 512 if stage >= 2 else 256
        for q in range(4):
            dma("sp", dbg[q * 1024:(q + 1) * 1024, 0:256], o_srcf[q], s_d, reads=["o_srcf"], writes=["dbg"])
            if stage >= 2:
                dma("sp", dbg[q * 1024:(q + 1) * 1024, 256:512], o_srcn[q], s_d, reads=[("o_srcn", q)], writes=["dbg"])
        dma("sp", stg[0:4, :], mod_all, s_d, reads=["mod_all"], writes=["stg"])
        dma("sp", dbgm, stg[0:4, :], s_d, reads=["stg"], writes=["dbgm"])
    else:
        mem.off = P_MARK - 16 * 1024 - 512
        small2 = mem.t([128], F32)
        EPS2 = small2[:, 120:121]
        g1B = mem.t([D], F32)
        u2T = mem.t([16, 1024], BF16)
        R_MARK = mem.off
        w_o = mem.t([16, D], BF16)
        yt_all = mem.t([8, D], BF16)
        ytn = [mem.t([D], BF16) for _ in range(2)]
        yT = [mem.t([16, 128], BF16) for _ in range(2)]
        xt = [mem.t([D], F32) for _ in range(2)]
        h1 = [mem.t([D], F32) for _ in range(2)]
        h1n = [mem.t([D], BF16) for _ in range(2)]
        ut2 = mem.t([1024], F32)
        s_p2 = dsem("p2", True)
        mflat = mod_all.rearrange("r n -> (r n)")
        dma("sp", g1B, mflat[4096:6144].partition_broadcast(128), s_p2, reads=["mod_all"], writes=["g1B", "ostage"])
        sc.op("dve", lambda e: e.memset(EPS2, EPS), writes=["eps2", "ostage"])
        sc.op("dve", lambda e: e.memset(small2[:, 0:64], 0.0), writes=["ss2all"])
        s_wo = dsem("wo", True)
        w_out_v = w_out.rearrange("(j p) n -> p j n", p=128)
        for dc in range(4):
            dma("pool", w_o[:, :, dc * 512:(dc + 1) * 512], w_out_v[:, :, dc * 512:(dc + 1) * 512], dsem("wo%d" % dc), writes=[("w_o", dc)])
        s_y = dsem("y")
        s_x = [dsem("x2a"), dsem("x2b")]
        s_h = [dsem("h1a"), dsem("h1b")]

        def rstd_act(ssx, rsx, n, kin, kout):
            sc.op("act", lambda e: e.activation(out=rsx, in_=ssx, func=AF.Ln, scale=1.0 / n, bias=EPS2), reads=[kin, "eps2"], writes=[kout])
            sc.op("act", lambda e: e.activation(out=rsx, in_=rsx, func=AF.Exp, scale=-0.5), reads=[kout], writes=[kout])

        def rstd_from(ssx, rsx, n, tag):
            rstd_act(ssx, rsx, n, tag + "ss", tag + "rs")

        sc.want_pid = True
        for r in range(4):
            for a in range(2):
                def ld_y(e, r=r, a=a):
                    src_all = o_allf if a == 0 else o_alln
                    return e.dma_start(out=yt_all[:, :, a * 1024 + r * 256:a * 1024 + (r + 1) * 256],
                                       in_=src_all[bass.ds(sc.rank_row + r * 1024, 1024), :].rearrange("(t p) c -> p t c", p=128))
                sc.op("pool", ld_y, reads=[("o_allf", q) for q in range(4)] + [("o_alln", q) for q in range(4)], writes=["yt"], dsem=s_y)

        def S1(n):
            p = n % 2
            yt = yt_all[:, n, :]
            for hf in range(2):
                ssx = small2[:, 2 * n + hf:2 * n + hf + 1]
                rsx = small2[:, 16 + 2 * n + hf:16 + 2 * n + hf + 1]
                cs = slice(hf * 1024, (hf + 1) * 1024)
                kss, krs = ("yss", n, hf), ("yrs", n, hf)
                sc.op("act", (lambda e, cs=cs, ssx=ssx: e.activation(out=ytn[p][:, cs], in_=yt[:, cs], func=AF.Square, accum_out=ssx)),
                      reads=["yt", "ss2all"], writes=[("ytn", p), kss])
                rstd_act(ssx, rsx, 1024, kss, krs)
                sc.op("act", (lambda e, cs=cs, rsx=rsx: e.activation(out=ytn[p][:, cs], in_=yt[:, cs], func=AF.Copy, scale=rsx)),
                      reads=["yt", krs], writes=[("ytn", p)])

        def transposes(src, tagsrc, evac):
            for half in range(2):
                pbv = PS[half][:, :].bitcast(BF16)
                for jj in range(8):
                    j = half * 8 + jj
                    sc.op("pe", (lambda e, pbv=pbv, jj=jj, j=j: e.transpose(pbv[:, jj * 128:(jj + 1) * 128],
                                                                            src[:, j * 128:(j + 1) * 128], ident)),
                          reads=[tagsrc, "ident"], writes=[psk(half)], signal=(jj == 7))
                evac(half, pbv.rearrange("p (a b) -> p a b", a=8))

        def S2(n):
            p = n % 2

            def ev_y(half, pv3):
                bb = betaT[:, half * 8:(half + 1) * 8].unsqueeze(2).to_broadcast([128, 8, 128])
                sc.op("dve", lambda e: e.tensor_tensor(out=yT[p][:, half * 8:(half + 1) * 8, :], in0=pv3, in1=bb, op=ALU.mult),
                      reads=[psk(half), "betaT"], writes=[("yT", p)])
            transposes(ytn[p], ("ytn", p), ev_y)

        def S3(n):
            p = n % 2
            for dc in range(4):
                for j in range(16):
                    sc.op("pe", (lambda e, dc=dc, j=j: e.matmul(PS[4 + dc][:, :], lhsT=yT[p][:, j, :], rhs=w_o[:, j, dc * 512:(dc + 1) * 512],
                                                                start=(j == 0), stop=(j == 15))),
                          reads=[("yT", p), ("w_o", dc)], writes=[psk(4 + dc)], signal=(j == 15))

        def S4(n):
            p = n % 2
            for dc in range(4):
                cs = slice(dc * 512, (dc + 1) * 512)
                sc.op("dve", (lambda e, dc=dc, cs=cs: e.tensor_tensor(out=h1[p][:, cs], in0=PS[4 + dc][:, :], in1=g1B[:, cs], op=ALU.mult)),
                      reads=[psk(4 + dc), "g1B"], writes=[("h1", p, dc)])
                sc.op("dve", (lambda e, cs=cs: e.tensor_tensor(out=h1[p][:, cs], in0=h1[p][:, cs], in1=xt[p][:, cs], op=ALU.add)),
                      reads=[("h1", p, dc), ("xt", p)], writes=[("h1", p, dc)])
            h1k = [("h1", p, dc) for dc in range(4)]
            dma("sp", h1_scr[n * 128:(n + 1) * 128, :], h1[p], s_h[p], reads=h1k, writes=[("h1scr", n)])

        def S5(n):
            p = n % 2
            h1k = [("h1", p, dc) for dc in range(4)]
            ssx = small2[:, 32 + n:33 + n]
            rsx = small2[:, 48 + n:49 + n]
            sc.op("act", lambda e: e.activation(out=h1n[p], in_=h1[p], func=AF.Square, accum_out=ssx),
                  reads=h1k + ["ss2all"], writes=[("h1n", p), ("hss", n)])
            rstd_act(ssx, rsx, D, ("hss", n), ("hrs", n))
            sc.op("act", lambda e: e.activation(out=h1n[p], in_=h1[p], func=AF.Copy, scale=rsx), reads=h1k + [("hrs", n)], writes=[("h1n", p)])

        def S6(n):
            p = n % 2

            def ev_u(half, pv3):
                a2b = A2[:, half * 8:(half + 1) * 8].unsqueeze(2).to_broadcast([128, 8, 128])
                b2b = B2[:, half * 8:(half + 1) * 8].unsqueeze(2).to_broadcast([128, 8, 128])
                ut3 = ut2.rearrange("p (a b) -> p a b", a=8)
                sc.op("dve", lambda e: e.tensor_tensor(out=ut3, in0=pv3, in1=a2b, op=ALU.mult), reads=[psk(half), "A2"], writes=["ut2"])
                sc.op("dve", lambda e: e.tensor_tensor(out=u2T[:, half * 8:(half + 1) * 8, n * 128:(n + 1) * 128], in0=ut3, in1=b2b, op=ALU.add),
                      reads=["ut2", "B2"], writes=["u2T"])
            transposes(h1n[p], ("h1n", p), ev_u)

        def ldx(n):
            dma("sp", xt[n % 2], x_my[n * 128:(n + 1) * 128, :], s_x[n % 2], writes=[("xt", n % 2)])
        ldx(0)
        ldx(1)
        S1(0)
        S1(1)
        S2(0)
        for n in range(8):
            if n + 2 < 8:
                S1(n + 2)
            if n + 1 < 8:
                S2(n + 1)
            S3(n)
            if n >= 1:
                S6(n - 1)
            S4(n)
            if n + 2 < 8:
                ldx(n + 2)
            S5(n)
        S6(7)
        def p2_tail():
            sc.barrier()
            mem.off = R_MARK
            aT = mem.t([NFC, 1024], BF16)
            g2B = mem.t([D], F32)
            fnB = mem.t([D], F32)
            s_p3 = dsem("p3", True)
            dma("sp", g2B, mflat[10240:12288].partition_broadcast(128), s_p3, reads=["mod_all"], writes=["g2B"])
            dma("sp", fnB, fnorm[0].partition_broadcast(128), s_p3, writes=["fnB"])
            WG_OFF = mem.off
            wg = [mem.t([16, 128], BF16) for _ in range(2)]
            wu = [mem.t([16, 128], BF16) for _ in range(2)]
            sg = [mem.t([512], F32) for _ in range(2)]
            wd = [mem.t([4, 512], BF16) for _ in range(3)]
            hb = [mem.t([512], F32) for _ in range(2)]
            ob = [mem.t([512], F32) for _ in range(2)]
            orow = mem.t([D], F32)
            ss3 = mem.t([32], F32)
            s_wg = [dsem("wg0"), dsem("wg1")]
            s_wu = [dsem("wu0"), dsem("wu1")]
            wgv = w_gate.rearrange("(j p) n -> p j n", p=128)
            wuv = w_up.rearrange("(j p) n -> p j n", p=128)
            for fc in range(NFC):
                sl = fc % 2
                dma("pool", wg[sl], wgv[:, :, fc * 128:(fc + 1) * 128], s_wg[sl], writes=[("wg", sl)])
                dma("pool", wu[sl], wuv[:, :, fc * 128:(fc + 1) * 128], s_wu[sl], writes=[("wu", sl)])
                for half in range(2):
                    bg = (fc % 2) * 4 + half * 2
                    bu = bg + 1
                    for j in range(16):
                        sc.op("pe", (lambda e, bg=bg, sl=sl, j=j, half=half: e.matmul(PS[bg][:, :], lhsT=wg[sl][:, j, :],
                                                                                      rhs=u2T[:, j, half * 512:(half + 1) * 512],
                                                                                      start=(j == 0), stop=(j == 15))),
                              reads=[("wg", sl), "u2T"], writes=[psk(bg)], signal=(j == 15))
                    for j in range(16):
                        sc.op("pe", (lambda e, bu=bu, sl=sl, j=j, half=half: e.matmul(PS[bu][:, :], lhsT=wu[sl][:, j, :],
                                                                                      rhs=u2T[:, j, half * 512:(half + 1) * 512],
                                                                                      start=(j == 0), stop=(j == 15))),
                              reads=[("wu", sl), "u2T"], writes=[psk(bu)], signal=(j == 15))
                    sc.op("act", (lambda e, bg=bg, half=half: e.activation(out=sg[half], in_=PS[bg][:, :], func=AF.Silu)),
                          reads=[psk(bg)], writes=[("sg", half)])
                    sc.op("dve", (lambda e, bu=bu, half=half, fc=fc: e.tensor_tensor(out=aT[:, fc, half * 512:(half + 1) * 512], in0=sg[half],
                                                                                     in1=PS[bu][:, :], op=ALU.mult)),
                          reads=[("sg", half), psk(bu)], writes=["aT"])
            if stage < 3.3:
                return
            s_wd = [dsem("wd%d" % i) for i in range(4)]
            s_hb = [dsem("hb0"), dsem("hb1")]
            s_ob = [dsem("ob0"), dsem("ob1")]
            sc.op("dve", lambda e: e.memset(ss3, 0.0), writes=["ss3"])
            cnt = 0
            WG = 4
            gcount = 0
            for dcn in range(4):
                for fg in range(NFC // WG):
                    sl = gcount % 3
                    gcount += 1
                    dma("pool", wd[sl], w_down[fg * WG * 128:(fg + 1) * WG * 128, dcn * 512:(dcn + 1) * 512].rearrange("(g p) n -> p g n", p=128),
                        s_wd[sl], writes=[("wd", sl)])
                    for g in range(WG):
                        fc = fg * WG + g
                        for tt in range(8):
                            sc.op("pe", (lambda e, tt=tt, fc=fc, sl=sl, g=g: e.matmul(PS[tt][:, :], lhsT=aT[:, fc, tt * 128:(tt + 1) * 128], rhs=wd[sl][:, g, :],
                                                                                      start=(fc == 0), stop=(fc == NFC - 1))),
                                  reads=["aT", ("wd", sl)], writes=[psk(tt)], signal=(fc == NFC - 1 or tt == 7))
                for tt in range(8):
                    s2 = cnt % 2
                    cnt += 1
                    dma("sp", hb[s2], h1_scr[tt * 128:(tt + 1) * 128, dcn * 512:(dcn + 1) * 512], s_hb[s2],
                        reads=[("h1scr", tt)], writes=[("hb", s2)])
                    sc.op("dve", (lambda e, tt=tt, s2=s2, dcn=dcn: e.tensor_tensor(out=ob[s2], in0=PS[tt][:, :],
                                                                                   in1=g2B[:, dcn * 512:(dcn + 1) * 512], op=ALU.mult)),
                          reads=[psk(tt), "g2B"], writes=[("ob", s2)])
                    sc.op("dve", (lambda e, s2=s2: e.tensor_tensor(out=ob[s2], in0=ob[s2], in1=hb[s2], op=ALU.add)),
                          reads=[("ob", s2), ("hb", s2)], writes=[("ob", s2)])
                    sc.op("act", (lambda e, s2=s2, tt=tt, dcn=dcn: e.activation(out=hb[s2], in_=ob[s2], func=AF.Square,
                                                                                accum_out=ss3[:, tt * 4 + dcn:tt * 4 + dcn + 1])),
                          reads=[("ob", s2), "ss3"], writes=[("hb", s2), "ss3"])
                    dma("sp", out[tt * 128:(tt + 1) * 128, dcn * 512:(dcn + 1) * 512], ob[s2], s_ob[s2],
                        reads=[("ob", s2)], writes=[("outpre", tt)])
            if stage < 3.4:
                return
            orows = [orow, mem.at(WG_OFF, [D], F32), mem.at(WG_OFF + 8192, [D], F32)]
            s_ld = [dsem("orl%d" % i) for i in range(3)]
            s_st = [dsem("ors%d" % i) for i in range(3)]

            def fin_load(tt):
                dma("pool", orows[tt % 3], out[tt * 128:(tt + 1) * 128, :], s_ld[tt % 3], reads=[("outpre", tt)], writes=[("orow", tt % 3)])
            fin_load(0)
            fin_load(1)
            for tt in range(8):
                if tt + 2 < 8:
                    fin_load(tt + 2)
                ob3 = orows[tt % 3]
                ssf = small2[:, 100 + tt:101 + tt]
                rsf = small2[:, 108 + tt:109 + tt]
                sc.op("dve", (lambda e, tt=tt, ssf=ssf: e.tensor_reduce(out=ssf, in_=ss3[:, tt * 4:(tt + 1) * 4], axis=mybir.AxisListType.X, op=ALU.add)),
                      reads=["ss3"], writes=[("fss", tt)])
                rstd_act(ssf, rsf, D, ("fss", tt), ("frs", tt))
                sc.op("dve", (lambda e, ob3=ob3, rsf=rsf: e.scalar_tensor_tensor(out=ob3, in0=ob3, scalar=rsf, in1=fnB, op0=ALU.mult, op1=ALU.mult)),
                      reads=[("orow", tt % 3), ("frs", tt), "fnB"], writes=[("orow", tt % 3)])
                dma("sp", out[tt * 128:(tt + 1) * 128, :], ob3, s_st[tt % 3], reads=[("orow", tt % 3)], writes=[("out", tt)])

        if stage >= 3.2:
            p2_tail()
        if stage < 3.4:
            sc.barrier()
            s_dd = dsem("dbgh", True)
            dma("sp", out, h1_scr, s_dd, reads=[], writes=["outdbg"])

    sc.barrier()
    with nc.Block() as block:
        sc.replay(block)
    nc._in_names = in_names
    return nc


def nsa_phase(nc, sc, mem, L):
    raise NotImplementedError


def ffn_phase(nc, sc, mem, L):
    raise NotImplementedError


def _consts():
    p = np.arange(128)
    ident = np.eye(128, dtype=np.float32)
    perm = np.zeros((128, 128), np.float32)
    perm[(p + 64) % 128, p] = 1.0
    tri = (p[None, :] >= p[:, None]).astype(np.float32)
    atri = (p[:, None] > p[None, :]).astype(np.float32)
    ones = np.ones((128, 128), np.float32)
    E = np.zeros((64, S), np.float32)
    E[np.arange(S) // 64, np.arange(S)] = 1.0
    maskc = np.zeros((128, 17, 128), np.float32)
    for ip in range(17):
        maskc[:, ip, :] = (16 * p[:, None] + 31 <= 128 * ip + p[None, :])
    selA = np.zeros((128, 32, 64), np.float32)
    selB = np.zeros((128, 32, 64), np.float32)
    n = np.arange(64)
    for i in range(32):
        cur = (128 * i + p) // 64
        forced = (n[None, :] == 0) | (n[None, :] == cur[:, None]) | (n[None, :] == cur[:, None] - 1)
        causal = n[None, :] <= cur[:, None]
        selA[:, i, :] = (causal & ~forced)
        selB[:, i, :] = np.where(causal, np.where(forced, 1e6, 0.0), -1e6)
    c = np.arange(256)
    ovl = (np.minimum(c[:, None] * 16 + 32, n[None, :] * 64 + 64) - np.maximum(c[:, None] * 16, n[None, :] * 64))
    ovl = np.clip(ovl, 0, None).astype(np.float32) / 16.0
    ovl[255] = 0.0
    ovl = ovl.reshape(2, 128, 64).transpose(1, 0, 2)
    freq = np.zeros((128, 4), np.float32)
    inv = (10000.0 ** (-np.arange(0, 128, 2, dtype=np.float32) / 128)).astype(np.float32)
    freq[:, 0] = np.concatenate([inv, inv])
    freq[:, 1] = np.concatenate([-np.ones(64), np.ones(64)])
    freq[:, 2] = np.pi
    return dict(c_ident=ident, c_perm=perm, c_tri=tri, c_atri=atri, c_triU=tri.copy(), c_ones=ones, c_E=E,
                c_maskc=maskc.reshape(128, -1), c_selA=selA.reshape(128, -1), c_selB=selB.reshape(128, -1),
                c_ovl=np.ascontiguousarray(ovl).reshape(128, -1), c_freq=freq)


def _tm(v):
    return np.ascontiguousarray(np.asarray(v).reshape(16, 128).T)


def make_in_maps(x, c, positions, w_ada, b_ada, norm_attn, norm_ffn, w_in, b_fgate, cmp_pos,
                 w_kc1, w_kc2, w_vc1, w_vc2, beta_fox, beta_nsa, w_out, w_gate, w_up, w_down, final_norm):
    f = lambda a: np.ascontiguousarray(np.asarray(a))
    x = f(x); c = f(c); positions = f(positions).astype(np.int32)
    w_in0 = f(w_in)[0]
    consts = _consts()
    shared = dict(
        nattn_t=_tm(f(norm_attn)[0]), nffn_t=_tm(f(norm_ffn)[0]), fnorm=f(final_norm).reshape(1, D),
        cposT=np.ascontiguousarray(f(cmp_pos)[0].T), w_kc1=f(w_kc1)[0], w_kc2=f(w_kc2)[0], w_vc1=f(w_vc1)[0], w_vc2=f(w_vc2)[0],
        beta_t=_tm(np.concatenate([f(beta_fox)[0], f(beta_nsa)[0]])),
        w_out=f(w_out)[0], w_gate=f(w_gate)[0], w_up=f(w_up)[0], w_down=f(w_down)[0], **consts)
    maps = []
    QF, KF, VF, FL, QN, KC, VC, KS, VS, KW, VW, GN = 0, 1024, 2048, 3072, 3080, 4104, 4360, 4616, 4872, 5128, 5384, 5640
    cmp_end = np.arange(255) * 16 + 31
    for core in range(8):
        b, k = core // 4, core % 4
        g = k // 2
        hs = [2 * k, 2 * k + 1]
        oth = [h for h in range(4 * g, 4 * g + 4) if h not in hs]
        cols = []
        for base in (QF, KF, VF):
            for h in hs:
                cols.append(np.arange(base + h * 128, base + (h + 1) * 128))
        cols.append(np.array([FL + hs[0], FL + hs[1]]))
        w_fox = np.ascontiguousarray(w_in0[:, np.concatenate(cols)])
        cols = []
        for h in hs + oth:
            cols.append(np.arange(QN + h * 128, QN + (h + 1) * 128))
        for base in (KS, KW, KC, VC, VS, VW):
            cols.append(np.arange(base + g * 128, base + (g + 1) * 128))
        for h in hs:
            cols.append(np.arange(GN + h * 3, GN + h * 3 + 3))
        w_nsa = np.ascontiguousarray(w_in0[:, np.concatenate(cols)])
        pe = np.zeros((1, 256), np.int32)
        pe[0, :255] = positions[b, cmp_end]
        m = dict(shared)
        m.update(
            x_b=x[b], x_my=np.ascontiguousarray(x[b, 1024 * k:1024 * (k + 1)]), c_t=_tm(c[b]),
            pos_b=positions[b].reshape(1, S), pos_end=pe,
            w_ada=np.ascontiguousarray(f(w_ada)[0][:, 3072 * k:3072 * (k + 1)]),
            b_ada=np.ascontiguousarray(f(b_ada)[0][3072 * k:3072 * (k + 1)]).reshape(1, 3072),
            w_fox=w_fox, b_fg=f(b_fgate)[0][hs].reshape(1, 2).astype(np.float32), w_nsa=w_nsa)
        maps.append(m)
    return maps


_NC_CACHE = {}


def kernel(**inputs):
    maps = make_in_maps(**inputs)
    if "nc" not in _NC_CACHE:
        _NC_CACHE["nc"] = build()
    nc = _NC_CACHE["nc"]
    maps = [{k: m[k] for k in nc._in_names} for m in maps]
    res = run_bass_kernel_spmd(nc, maps, core_ids=list(range(8)))
    outp = np.zeros((2, S, D), np.float32)
    for core in range(8):
        b, k = core // 4, core % 4
        outp[b, 1024 * k:1024 * (k + 1)] = res.results[core]["out"]
    return outp
```

```python
import numpy as np
import concourse.bass as bass
import concourse.mybir as mybir
from concourse.bass_utils import run_bass_kernel_spmd

F32 = mybir.dt.float32
BF16 = mybir.dt.bfloat16
I32 = mybir.dt.int32
U8 = mybir.dt.uint8
AF = mybir.ActivationFunctionType
ALU = mybir.AluOpType

D = 2048
S = 4096
NT = 32
DFF = 5632
NFC = DFF // 128
EPS = 1e-6
SCALE = 128 ** -0.5
BIG = 30000.0
NFOX = 770
NNSA_FM = 1024
NNSA_TM = 262
NNSA = NNSA_FM + NNSA_TM
TWO_PI = 2.0 * np.pi


class Sem:
    def __init__(self, nc, name, group=False):
        self.h = nc.alloc_semaphore(name)
        self.count = 0
        self.group = group


class Sched:
    ENG = ("pe", "act", "dve", "pool", "sp")

    def __init__(self, nc):
        self.nc = nc
        self.streams = {e: [] for e in self.ENG}
        self.esem = {e: Sem(nc, "es_" + e) for e in self.ENG}
        self.waited = {e: {} for e in self.ENG}
        self.lastw = {}
        self.lastr = {}
        self.all_sems = list(self.esem.values())
        self.want_pid = False
        self.pid = None

    def new_sem(self, name, group=False):
        s = Sem(self.nc, name, group)
        self.all_sems.append(s)
        return s

    def op(self, eng, fn, reads=(), writes=(), dsem=None, signal=True, inc=None):
        waits = {}

        def need(sc):
            s, c = sc
            if s.group:
                c = s.count
            if c > waits.get(s, 0):
                waits[s] = c
        own0 = self.esem[eng]
        for k in reads:
            w = self.lastw.get(k)
            if w:
                need(w)
            if isinstance(k, tuple) and k[0] == "ps":
                for s, c in self.lastr.get(k, {}).items():
                    if s is not own0:
                        need((s, c))
        for k in writes:
            w = self.lastw.get(k)
            if w:
                need(w)
            for s, c in self.lastr.get(k, {}).items():
                need((s, c))
        own = self.esem[eng]
        final = []
        for s, c in waits.items():
            if s is own and eng == "pe":
                continue
            if self.waited[eng].get(s, 0) >= c:
                continue
            self.waited[eng][s] = c
            final.append((s, c))
        if dsem is not None:
            step = 16 if inc is None else inc
            dsem.count += step
            mark = (dsem, dsem.count)
            sig = (dsem, step)
        else:
            if signal:
                own.count += 1
                mark = (own, own.count)
                sig = (own, 1)
            else:
                mark = (own, own.count + 1)
                sig = None
        for k in writes:
            self.lastw[k] = mark
            self.lastr[k] = {}
        for k in reads:
            d = self.lastr.setdefault(k, {})
            if d.get(mark[0], 0) < mark[1]:
                d[mark[0]] = mark[1]
        self.streams[eng].append((final, fn, sig))

    def barrier(self):
        snap = [(s, s.count) for s in self.all_sems if s.count > 0]
        for e in self.ENG:
            final = []
            for s, c in snap:
                if s is self.esem[e]:
                    continue
                if self.waited[e].get(s, 0) >= c:
                    continue
                self.waited[e][s] = c
                final.append((s, c))
            if final:
                self.streams[e].append((final, None, None))
        self.lastw = {}
        self.lastr = {}

    def replay(self, block):
        def run(engh, stream):
            for waits, fn, sig in stream:
                for s, c in waits:
                    engh.wait_ge(s.h, c)
                if fn is None:
                    continue
                ins = fn(engh)
                if sig is not None:
                    ins.then_inc(sig[0].h, sig[1])
        st = self.streams

        @block.tensor
        def _(e):
            run(e, st["pe"])

        @block.scalar
        def _(e):
            run(e, st["act"])

        @block.vector
        def _(e):
            run(e, st["dve"])

        @block.gpsimd
        def _(e):
            if self.want_pid:
                self.pid = e.partition_id()
                self.rank_row = e.snap((self.pid % 4) * S, min_val=0, max_val=3 * S)
            run(e, st["pool"])

        @block.sync
        def _(e):
            run(e, st["sp"])


class Mem:
    def __init__(self, nc, total):
        self.raw = nc.alloc_sbuf_tensor("raw", [128, total], U8)
        self.total = total
        self.off = 0

    def at(self, off, shape, dt):
        save = self.off
        self.off = off
        v = self.t(shape, dt)
        self.off = save
        return v

    def t(self, shape, dt, parts=128):
        n = 1
        for s in shape:
            n *= s
        nb = n * (2 if dt == BF16 else 4)
        nb_al = (nb + 63) // 64 * 64
        assert self.off + nb_al <= self.total, ("SBUF overflow", self.off, nb_al, self.total)
        v = self.raw[:, self.off:self.off + nb].bitcast(dt)
        self.off += nb_al
        if len(shape) == 2:
            v = v.rearrange("p (a b) -> p a b", a=shape[0])
        elif len(shape) == 3:
            v = v.rearrange("p (a b c) -> p a b c", a=shape[0], b=shape[1])
        elif len(shape) == 4:
            v = v.rearrange("p (a b c d) -> p a b c d", a=shape[0], b=shape[1], c=shape[2])
        return v


def build(stage=99):
    nc = bass.Bass("TRN2", target_bir_lowering=False)
    sc = Sched(nc)

    in_names = []
    NSA_ONLY = ("pos_b", "pos_end", "w_nsa", "cposT", "w_kc1", "w_kc2", "w_vc1", "w_vc2", "c_perm", "c_atri", "c_E", "c_maskc",
                "c_selA", "c_selB", "c_ovl", "c_freq")
    FFN_ONLY = ("x_my", "fnorm", "w_out", "w_gate", "w_up", "w_down")

    def din(name, shape, dt=F32):
        if (name in NSA_ONLY and stage < 2) or (name in FFN_ONLY and stage < 3):
            return None
        if name in ("w_gate", "w_up") and stage < 3.2:
            return None
        if name == "w_down" and stage < 3.3:
            return None
        in_names.append(name)
        return nc.dram_tensor(name, list(shape), dt, kind="ExternalInput").ap()

    x_b = din("x_b", [S, D])
    x_my = din("x_my", [1024, D])
    c_t = din("c_t", [128, 16])
    pos_b = din("pos_b", [1, S], I32)
    pos_end = din("pos_end", [1, 256], I32)
    w_ada = din("w_ada", [D, 3072])
    b_ada = din("b_ada", [1, 3072])
    nattn_t = din("nattn_t", [128, 16])
    nffn_t = din("nffn_t", [128, 16])
    fnorm = din("fnorm", [1, D])
    w_fox = din("w_fox", [D, NFOX])
    b_fg = din("b_fg", [1, 2])
    w_nsa = din("w_nsa", [D, NNSA])
    cposT = din("cposT", [128, 32])
    w_kc1 = din("w_kc1", [4096, 256])
    w_kc2 = din("w_kc2", [256, 128])
    w_vc1 = din("w_vc1", [4096, 256])
    w_vc2 = din("w_vc2", [256, 128])
    beta_t = din("beta_t", [128, 16])
    w_out = din("w_out", [D, D])
    w_gate = din("w_gate", [D, DFF])
    w_up = din("w_up", [D, DFF])
    w_down = din("w_down", [DFF, D])
    c_ident = din("c_ident", [128, 128])
    c_perm = din("c_perm", [128, 128])
    c_tri = din("c_tri", [128, 128])
    c_atri = din("c_atri", [128, 128])
    c_triU = din("c_triU", [128, 128])
    c_ones = din("c_ones", [128, 128])
    c_E = din("c_E", [64, S])
    c_maskc = din("c_maskc", [128, 17 * 128])
    c_selA = din("c_selA", [128, 32 * 64])
    c_selB = din("c_selB", [128, 32 * 64])
    c_ovl = din("c_ovl", [128, 2 * 64])
    c_freq = din("c_freq", [128, 4])

    out = nc.dram_tensor("out", [1024, D], F32, kind="ExternalOutput").ap()
    mod_src = nc.dram_tensor("mod_src", [1, 3072], F32).ap()
    mod_all = nc.dram_tensor("mod_all", [4, 3072], F32).ap()
    o_srcf = [nc.dram_tensor("o_srcf%d" % q, [1024, 256], BF16).ap() for q in range(4)]
    o_srcn = [nc.dram_tensor("o_srcn%d" % q, [1024, 256], BF16).ap() for q in range(4)]
    o_allf = nc.dram_tensor("o_allf", [4 * S, 256], BF16).ap()
    o_alln = nc.dram_tensor("o_alln", [4 * S, 256], BF16).ap()
    RG = [[0, 1, 2, 3], [4, 5, 6, 7]]
    h1_scr = nc.dram_tensor("h1_scr", [1024, D], F32).ap()

    TOTAL = 207 * 1024
    mem = Mem(nc, TOTAL)
    PS = [nc.alloc_psum_tensor("ps%d" % i, [128, 512], F32) for i in range(8)]

    def psk(b):
        return ("ps", b)

    dsem_cnt = [0]

    def dsem(name, group=False):
        dsem_cnt[0] += 1
        return sc.new_sem("d_%s_%d" % (name, dsem_cnt[0]), group)

    sem_by_q = {}

    def dma(eng, out_ap, in_ap, sem, reads=(), writes=(), slow=False):
        key = (id(sem), eng)
        if key not in sem_by_q:
            first = not any(k[0] == id(sem) for k in sem_by_q)
            sem_by_q[key] = sem if first else sc.new_sem("dq%d" % len(sem_by_q), sem.group)
        sem = sem_by_q[key]
        if slow:
            fn = lambda e: e.dma_start(out=out_ap, in_=in_ap, allow_slow_non_contiguous=True)
        else:
            fn = lambda e: e.dma_start(out=out_ap, in_=in_ap)
        sc.op(eng, fn, reads=reads, writes=writes, dsem=sem)

    ident = mem.t([128], BF16)
    perm = mem.t([128], BF16)
    tri = mem.t([128], BF16)
    atri = mem.t([128], BF16)
    identF = mem.t([128], F32)
    A1 = mem.t([16], F32)
    B1 = mem.t([16], F32)
    A2 = mem.t([16], F32)
    B2 = mem.t([16], F32)
    betaT = mem.t([16], F32)
    ostage = mem.t([NT, 256], BF16)
    small = mem.t([64], F32)
    EPSB = small[:, 32:33]
    mem_small_ss = mem.t([NT], F32)
    mem_small_rs = mem.t([NT], F32)
    P_MARK = mem.off

    s_const = dsem("const", True)
    dma("pool", ident, c_ident, s_const, writes=["ident"])
    if c_perm is not None:
        dma("pool", perm, c_perm, s_const, writes=["perm"])
    dma("pool", tri, c_tri, s_const, writes=["tri"])
    if c_atri is not None:
        dma("pool", atri, c_atri, s_const, writes=["atri"])
    dma("sp", identF, c_ident, s_const, writes=["identF"])
    dma("sp", betaT, beta_t, s_const, writes=["betaT"])
    sc.op("dve", lambda e: e.memset(EPSB, EPS), writes=["epsb"])

    ct = mem.t([16], F32)
    csil = mem.t([16], BF16)
    wa = [mem.t([16, 512], BF16) for _ in range(2)]
    modrow = mem.t([3072], F32, parts=1)
    badar = mem.t([3072], F32, parts=1)
    modrows = mem.t([128], F32)
    modT = mem.t([96], F32)
    nat = mem.t([16], F32)
    nft = mem.t([16], F32)

    s_p0 = dsem("p0", True)
    dma("sp", ct, c_t, s_p0, writes=["ct"])
    dma("sp", badar[0:1, :], b_ada, s_p0, writes=["badar"])
    dma("sp", nat, nattn_t, s_p0, writes=["nat"])
    dma("sp", nft, nffn_t, s_p0, writes=["nft"])
    sc.op("act", lambda e: e.activation(out=csil, in_=ct, func=AF.Silu), reads=["ct"], writes=["csil"])
    s_wa = [dsem("wa0"), dsem("wa1")]
    w_ada_v = w_ada.rearrange("(j p) n -> p j n", p=128)
    for nn in range(6):
        sl = nn % 2
        dma("pool", wa[sl], w_ada_v[:, :, nn * 512:(nn + 1) * 512], s_wa[sl], writes=[("wa", sl)])
        pb = PS[nn % 2]
        for j in range(16):
            sc.op("pe", (lambda e, pb=pb, sl=sl, j=j: e.matmul(pb[0:1, :], lhsT=csil[:, j:j + 1], rhs=wa[sl][:, j, :],
                                                               start=(j == 0), stop=(j == 15))),
                  reads=["csil", ("wa", sl)], writes=[psk(nn % 2)], signal=(j == 15))
        sc.op("dve", (lambda e, pb=pb, nn=nn: e.tensor_tensor(out=modrow[0:1, nn * 512:(nn + 1) * 512], in0=pb[0:1, :],
                                                              in1=badar[0:1, nn * 512:(nn + 1) * 512], op=ALU.add)),
              reads=[psk(nn % 2), "badar"], writes=["modrow"])
    s_mod = dsem("mod")
    dma("sp", mod_src, modrow[0:1, :], s_mod, reads=["modrow"], writes=["mod_src"])
    s_cc = sc.new_sem("cc0")
    sc.op("pool", lambda e: e.collective_compute("AllGather", ALU.bypass, replica_groups=[[0, 1, 2, 3], [4, 5, 6, 7]],
                                                 ins=[mod_src], outs=[mod_all]),
          reads=["mod_src"], writes=["mod_all"], dsem=s_cc, inc=1)
    mod_flat = mod_all.rearrange("r (a p) -> (r a) p", p=128)
    dma("sp", modrows[0:96, :], mod_flat, s_mod, reads=["mod_all"], writes=["modrows"])
    sc.op("pe", lambda e: e.matmul(PS[2][:, 0:96], lhsT=modrows[0:96, :], rhs=identF[0:96, 0:96], start=True, stop=True),
          reads=["modrows", "identF"], writes=[psk(2)])
    sc.op("dve", lambda e: e.tensor_copy(out=modT, in_=PS[2][:, 0:96]), reads=[psk(2)], writes=["modT"])
    sc.op("dve", lambda e: e.scalar_tensor_tensor(out=A1, in0=modT[:, 16:32], scalar=1.0, in1=nat, op0=ALU.add, op1=ALU.mult),
          reads=["modT", "nat"], writes=["A1"])
    sc.op("dve", lambda e: e.tensor_copy(out=B1, in_=modT[:, 0:16]), reads=["modT"], writes=["B1"])
    sc.op("dve", lambda e: e.scalar_tensor_tensor(out=A2, in0=modT[:, 64:80], scalar=1.0, in1=nft, op0=ALU.add, op1=ALU.mult),
          reads=["modT", "nft"], writes=["A2"])
    sc.op("dve", lambda e: e.tensor_copy(out=B2, in_=modT[:, 48:64]), reads=["modT"], writes=["B2"])
    sc.barrier()
    mem.off = P_MARK

    xs = [mem.t([D], F32) for _ in range(2)]
    xn = mem.t([D], BF16)
    utmp = mem.t([1024], F32)
    uTb = [mem.t([16, 512], BF16) for _ in range(2)]
    U_MARK = mem.off
    s_xs = [dsem("xs0"), dsem("xs1")]
    ss = small[:, 0:1]
    rstd = small[:, 1:2]
    xcount = [0]

    ssall = mem_small_ss
    rsall = mem_small_rs
    xnb = [xn, ostage[:, 8:16, :].rearrange("p a b -> p (a b)")]
    sqjunk = ostage[:, 0:8, :].rearrange("p a b -> p (a b)")

    def tile_dma(n):
        sl = n % 2
        dma("sp", xs[sl], x_b[n * 128:(n + 1) * 128, :], s_xs[sl], writes=[("xs", sl)])

    def tile_act(n):
        sl = n % 2
        xb = xnb[sl]
        ssc = ssall[:, n:n + 1]
        rsc = rsall[:, n:n + 1]
        sc.op("act", lambda e: e.activation(out=sqjunk, in_=xs[sl], func=AF.Square, accum_out=ssc),
              reads=[("xs", sl), "ssall"], writes=[("ss", n)])
        sc.op("act", lambda e: e.activation(out=rsc, in_=ssc, func=AF.Ln, scale=1.0 / D, bias=EPSB), reads=[("ss", n), "epsb"], writes=[("rs", n)])
        sc.op("act", lambda e: e.activation(out=rsc, in_=rsc, func=AF.Exp, scale=-0.5), reads=[("rs", n)], writes=[("rs", n)])
        sc.op("act", lambda e: e.activation(out=xb, in_=xs[sl], func=AF.Copy, scale=rsc),
              reads=[("xs", sl), ("rs", n)], writes=[("xn", sl)])

    def tile_pe(n):
        sl = n % 2
        xb = xnb[sl]
        ub = (n // 4) % 2
        tl = n % 4
        for half in range(2):
            pb = PS[half]
            pbv = pb[:, :].bitcast(BF16)
            for jj in range(8):
                j = half * 8 + jj
                sc.op("pe", (lambda e, pbv=pbv, jj=jj, j=j: e.transpose(pbv[:, jj * 128:(jj + 1) * 128],
                                                                        xb[:, j * 128:(j + 1) * 128], ident)),
                      reads=[("xn", sl), "ident"], writes=[psk(half)], signal=(jj == 7))
            pv3 = pbv.rearrange("p (a b) -> p a b", a=8)
            a1b = A1[:, half * 8:(half + 1) * 8].unsqueeze(2).to_broadcast([128, 8, 128])
            b1b = B1[:, half * 8:(half + 1) * 8].unsqueeze(2).to_broadcast([128, 8, 128])
            ut3 = utmp.rearrange("p (a b) -> p a b", a=8)
            sc.op("dve", (lambda e, ut3=ut3, pv3=pv3, a1b=a1b: e.tensor_tensor(out=ut3, in0=pv3, in1=a1b, op=ALU.mult)),
                  reads=[psk(half), "A1"], writes=["utmp"])
            sc.op("dve", (lambda e, ut3=ut3, b1b=b1b, half=half: e.tensor_tensor(
                out=uTb[ub][:, half * 8:(half + 1) * 8, tl * 128:(tl + 1) * 128], in0=ut3, in1=b1b, op=ALU.add)),
                reads=["utmp", "B1"], writes=[("uT", ub)])

    def run_projection(groups_of, nst=8, after_prologue=None):
        ntile = nst * 4
        sc.op("pool", lambda e: e.memset(ssall, 0.0), writes=["ssall"])

        def slot(n):
            if n + 2 < ntile:
                tile_dma(n + 2)
            if n + 1 < ntile:
                tile_act(n + 1)
            tile_pe(n)
        tile_dma(0)
        tile_dma(1)
        tile_act(0)
        for n in range(4):
            slot(n)
        if after_prologue is not None:
            after_prologue()
        for st in range(nst):
            groups = groups_of(st)
            ng = len(groups)
            for k in range(4):
                n = 4 * (st + 1) + k
                if n < ntile:
                    slot(n)
                for g in groups[(k * ng) // 4:((k + 1) * ng) // 4]:
                    g()

    def proj_fm(W, col0, dst_fn, st, bank, post=None):
        pb = PS[bank]
        uT = uTb[st % 2]
        for j in range(16):
            sc.op("pe", (lambda e, j=j: e.matmul(pb[:, :], lhsT=W[:, j, col0:col0 + 128], rhs=uT[:, j, :],
                                                 start=(j == 0), stop=(j == 15))),
                  reads=["W", ("uT", st % 2)], writes=[psk(bank)], signal=(j == 15))
        dst_fn(pb)

    if stage >= 0.1:
        Wf = mem.t([16, NFOX], BF16)
        WN_BYTES = (16 * NNSA * 2 + 63) // 64 * 64
        mem.off = max(mem.off, U_MARK + WN_BYTES)
        qT = mem.t([2, S], BF16)
        kT = mem.t([2, S], BF16)
        Vf = mem.t([NT, 2, 130], BF16)
        nlf = mem.t([NT, 2], F32)
        bfg = mem.t([2], F32)
        triU = mem.t([128], F32)
        onesF = mem.t([128], F32)
        ncw = mem.t([NT, 2], F32)
        tot = mem.t([NT, 2], F32)
        offs = mem.t([NT, 2], F32)
        ncum = mem.t([NT, 2], F32)
        ncref = mem.t([NT, 2], F32)
        Tb = mem.t([2, NT, NT], F32)
        pTf = mem.t([12, 128], BF16)
        ftmp = mem.t([8], F32)
        s_w = dsem("wf", True)
        dma("pool", Wf, w_fox.rearrange("(j p) n -> p j n", p=128), s_w, writes=["W"])
        dma("sp", bfg, b_fg[0].partition_broadcast(128), s_w, writes=["bfg"])
        dma("sp", triU, c_triU, s_w, writes=["triU"])
        dma("sp", onesF, c_ones, s_w, writes=["onesF"])
        sc.op("pool", lambda e: e.memset(Vf[:, :, :, 128:129], 1.0), writes=["Vf"])
        sc.op("pool", lambda e: e.memset(Vf[:, :, :, 129:130], 0.0), writes=["Vf"])
        def fox_groups(st):
            groups = []
            uT = uTb[st % 2]
            for cc in range(4):
                bank = 2 + cc % 2
                dstT = qT if cc < 2 else kT
                hh = cc % 2

                def g_fm(cc=cc, bank=bank, dstT=dstT, hh=hh):
                    def ev(pb):
                        sc.op("act", lambda e: e.copy(out=dstT[:, hh, st * 512:(st + 1) * 512], in_=pb[:, :]),
                              reads=[psk(bank)], writes=["qkT"])
                    proj_fm(Wf, cc * 128, ev, st, bank)
                groups.append(g_fm)
            for tl in range(4):
                def g_tm(tl=tl):
                    t = st * 4 + tl
                    bank = 4 + tl % 2
                    pb = PS[bank]
                    for j in range(16):
                        sc.op("pe", (lambda e, j=j: e.matmul(pb[:, 0:258], lhsT=uT[:, j, tl * 128:(tl + 1) * 128],
                                                             rhs=Wf[:, j, 512:770], start=(j == 0), stop=(j == 15))),
                              reads=["W", ("uT", st % 2)], writes=[psk(bank)], signal=(j == 15))
                    sc.op("act", lambda e: e.copy(out=Vf[:, t, :, 0:128], in_=pb[:, 0:256].rearrange("p (h d) -> p h d", h=2)),
                          reads=[psk(bank)], writes=["Vf"])
                    f0 = ftmp[:, 2 * (tl % 2):2 * (tl % 2) + 2]
                    fk = ("f0", tl % 2)
                    sc.op("dve", lambda e: e.tensor_tensor(out=f0, in0=pb[:, 256:258], in1=bfg, op=ALU.add),
                          reads=[psk(bank), "bfg"], writes=[fk])
                    sc.op("act", lambda e: e.activation(out=f0, in_=f0, func=AF.Exp, scale=-1.0), reads=[fk], writes=[fk])
                    sc.op("act", lambda e: e.activation(out=nlf[:, t, :], in_=f0, func=AF.Ln, bias=1.0), reads=[fk], writes=["nlf"])
                groups.append(g_tm)
            return groups
        run_projection(fox_groups)
        Wn_pref = None
        if stage >= 2:
            Wn_pref = mem.at(U_MARK, [16, NNSA], BF16)
            s_wn = dsem("wnpre")
            dma("pool", Wn_pref, w_nsa.rearrange("(j p) n -> p j n", p=128), s_wn, writes=["W"])
        def fox_cum():
            nlf2 = nlf.rearrange("p a b -> p (a b)")
            sc.op("pe", lambda e: e.matmul(PS[0][:, 0:64], lhsT=triU, rhs=nlf2, start=True, stop=True),
                  reads=["triU", "nlf"], writes=[psk(0)])
            sc.op("pe", lambda e: e.matmul(PS[1][:, 0:64], lhsT=onesF, rhs=nlf2, start=True, stop=True),
                  reads=["onesF", "nlf"], writes=[psk(1)])
            sc.op("dve", lambda e: e.tensor_copy(out=ncw.rearrange("p a b -> p (a b)"), in_=PS[0][:, 0:64]),
                  reads=[psk(0)], writes=["ncw"])
            sc.op("dve", lambda e: e.tensor_copy(out=tot.rearrange("p a b -> p (a b)"), in_=PS[1][:, 0:64]),
                  reads=[psk(1)], writes=["tot"])
            sc.op("dve", lambda e: e.memset(offs[:, 0, :], 0.0), writes=["offs"])
            for c in range(1, NT):
                sc.op("dve", (lambda e, c=c: e.tensor_tensor(out=offs[:, c, :], in0=offs[:, c - 1, :], in1=tot[:, c - 1, :], op=ALU.add)),
                      reads=["offs", "tot"], writes=["offs"])
            sc.op("dve", lambda e: e.tensor_tensor(out=ncum, in0=ncw, in1=offs, op=ALU.add), reads=["ncw", "offs"], writes=["ncum"])
            sc.op("dve", lambda e: e.scalar_tensor_tensor(out=ncref, in0=tot, scalar=0.5, in1=offs, op0=ALU.mult, op1=ALU.add),
                  reads=["tot", "offs"], writes=["ncref"])
            for h in range(2):
                for i in range(NT):
                    sc.op("dve", (lambda e, h=h, i=i: e.tensor_scalar(out=Tb[:, h, i, 0:i + 1], in0=ncum[:, 0:i + 1, h],
                                                                      scalar1=ncref[:, i, h:h + 1], scalar2=None, op0=ALU.subtract)),
                          reads=["ncum", "ncref"], writes=["Tb"])

        def fox_attn():
            pcount = [0]
            steps = [(h, I, j) for h in range(2) for I in range(8) for j in range(4 * I + 4)]
            LOOK, NB = 3, 4

            def qk(n):
                h, I, j = steps[n]
                sb = n % NB
                sc.op("pe", (lambda e, sb=sb, h=h, j=j, I=I: e.matmul(PS[sb][:, :], lhsT=kT[:, h, j * 128:(j + 1) * 128],
                                                                      rhs=qT[:, h, I * 512:(I + 1) * 512], start=True, stop=True)),
                      reads=["qkT"], writes=[psk(sb)])
            for n in range(min(LOOK, len(steps))):
                qk(n)
            for n, (h, I, j) in enumerate(steps):
                if n + LOOK < len(steps):
                    qk(n + LOOK)
                sb = n % NB
                for iq in range(4):
                    i = 4 * I + iq
                    if j > i:
                        continue
                    slot = pcount[0] % 12
                    pcount[0] += 1
                    sc.op("act", (lambda e, sb=sb, slot=slot, iq=iq, h=h, i=i, j=j: e.activation(
                        out=pTf[:, slot, :], in_=PS[sb][:, iq * 128:(iq + 1) * 128], func=AF.Exp,
                        bias=Tb[:, h, i, j:j + 1], scale=SCALE)),
                        reads=[psk(sb), "Tb"], writes=[("pTf", slot)])
                    if j == i:
                        sc.op("dve", (lambda e, slot=slot: e.tensor_tensor(out=pTf[:, slot, :], in0=pTf[:, slot, :], in1=tri, op=ALU.mult)),
                              reads=[("pTf", slot), "tri"], writes=[("pTf", slot)])
                    sc.op("pe", (lambda e, slot=slot, iq=iq, h=h, j=j, i=i: e.matmul(
                        PS[4 + iq][:, 0:130], lhsT=pTf[:, slot, :], rhs=Vf[:, j, h, :], start=(j == 0), stop=(j == i))),
                        reads=[("pTf", slot), "Vf"], writes=[psk(4 + iq)], signal=(j == i))
                    if j == i:
                        rs = small[:, 8 + iq:9 + iq]
                        sc.op("dve", (lambda e, iq=iq, rs=rs: e.reciprocal(out=rs, in_=PS[4 + iq][:, 128:129])),
                              reads=[psk(4 + iq)], writes=[("rs", iq)])
                        sc.op("dve", (lambda e, iq=iq, rs=rs, i=i, h=h: e.tensor_scalar(
                            out=ostage[:, i, h * 128:(h + 1) * 128], in0=PS[4 + iq][:, 0:128], scalar1=rs, scalar2=None, op0=ALU.mult)),
                            reads=[psk(4 + iq), ("rs", iq)], writes=["ostage"])
        if stage >= 0.6:
            fox_cum()
        if stage >= 1:
            fox_attn()
        s_of = dsem("osrc_f", True)
        for q in range(4):
            dma("sp", o_srcf[q].rearrange("(t p) f -> p t f", p=128), ostage[:, 8 * q:8 * q + 8, :], s_of,
                reads=["ostage"], writes=["o_srcf"])
        sc.barrier()
        mem.off = U_MARK

    if stage >= 2:
        Wn = mem.t([16, NNSA], BF16)
        W_MARK = mem.off
        qn = mem.t([4, S], BF16)
        ksT = mem.t([S], BF16)
        kwT = mem.t([S], BF16)
        kcr = mem.t([S], BF16)
        vcr = mem.t([S], BF16)
        Vsw = mem.t([NT, 2, 130], BF16)
        gat = mem.t([NT, 6], F32)
        posf = mem.t([512], F32)
        cosT = mem.t([512], F32)
        sinT = mem.t([512], F32)
        rtmp = mem.t([512], BF16)
        rt1 = mem.t([512], F32)
        rt2 = mem.t([512], F32)
        posi = rt2.bitcast(I32)
        kint = posf.bitcast(I32)
        frq = mem.t([4], F32)
        s_w = dsem("wn", True)
        def fox_exchange():
            if stage < 3:
                return
            s_ccf = sc.new_sem("ccf", True)
            for q in range(4):
                sc.op("pool", (lambda e, q=q: e.collective_compute("AllGather", ALU.bypass, replica_groups=RG,
                                                                   ins=[o_srcf[q]], outs=[o_allf[q * S:(q + 1) * S, :]])),
                      reads=[], writes=[("o_allf", q)], dsem=s_ccf, inc=1)
        dma("sp", frq, c_freq, s_w, writes=["frq"])
        sc.op("pool", lambda e: e.memset(Vsw[:, :, :, 128:129], 1.0), writes=["Vsw"])
        sc.op("pool", lambda e: e.memset(Vsw[:, :, :, 129:130], 0.0), writes=["Vsw"])
        s_pos = dsem("pos")

        def rope_tables(src_ap, n, tag):
            dma("sp", posi[:, 0:n], src_ap.partition_broadcast(128), s_pos, writes=["rt2"])
            sc.op("dve", lambda e: e.tensor_copy(out=posf[:, 0:n], in_=posi[:, 0:n]), reads=["rt2"], writes=["posf"])
            sc.op("dve", lambda e: e.tensor_scalar(out=rt1[:, 0:n], in0=posf[:, 0:n], scalar1=frq[:, 0:1], scalar2=None, op0=ALU.mult),
                  reads=["posf", "frq"], writes=["rt1"])
            for dstT, addc, scl in ((sinT, 0.0, frq[:, 1:2]), (cosT, float(np.pi / 2), 1.0)):
                sc.op("dve", (lambda e, addc=addc: e.tensor_scalar(out=rt2[:, 0:n], in0=rt1[:, 0:n], scalar1=addc, scalar2=float(1.0 / TWO_PI),
                                                                   op0=ALU.add, op1=ALU.mult)), reads=["rt1"], writes=["rt2"])
                sc.op("dve", lambda e: e.tensor_copy(out=kint[:, 0:n], in_=rt2[:, 0:n]), reads=["rt2"], writes=["posf"])
                sc.op("dve", lambda e: e.tensor_copy(out=rt2[:, 0:n], in_=kint[:, 0:n]), reads=["posf"], writes=["rt2"])
                sc.op("dve", lambda e: e.scalar_tensor_tensor(out=rt2[:, 0:n], in0=rt2[:, 0:n], scalar=float(-TWO_PI), in1=rt1[:, 0:n],
                                                              op0=ALU.mult, op1=ALU.add), reads=["rt2", "rt1"], writes=["rt2"])
                sc.op("dve", (lambda e, addc=addc: e.tensor_scalar(out=rt2[:, 0:n], in0=rt2[:, 0:n], scalar1=addc, scalar2=float(np.pi),
                                                                   op0=ALU.add, op1=ALU.min)), reads=["rt2"], writes=["rt2"])
                sc.op("dve", lambda e: e.tensor_scalar(out=rt2[:, 0:n], in0=rt2[:, 0:n], scalar1=float(-np.pi), scalar2=None, op0=ALU.max),
                      reads=["rt2"], writes=["rt2"])
                sc.op("act", (lambda e, dstT=dstT, scl=scl: e.activation(out=dstT[:, 0:n], in_=rt2[:, 0:n], func=AF.Sin, scale=scl)),
                      reads=["rt2", "frq"], writes=["sinT" if dstT is sinT else "cosT"])

        def rope_evac(pb, bank, dst, n):
            sc.op("act", lambda e: e.copy(out=rtmp[:, 0:n], in_=pb[:, 0:n]), reads=[psk(bank)], writes=["rtmp"])
            sc.op("pe", lambda e: e.matmul(PS[7][:, 0:n], lhsT=perm, rhs=rtmp[:, 0:n], start=True, stop=True),
                  reads=["rtmp", "perm"], writes=[psk(7)])
            sc.op("dve", lambda e: e.tensor_tensor(out=rt1[:, 0:n], in0=pb[:, 0:n], in1=cosT[:, 0:n], op=ALU.mult),
                  reads=[psk(bank), "cosT"], writes=["rt1"])
            sc.op("dve", lambda e: e.tensor_tensor(out=rt2[:, 0:n], in0=PS[7][:, 0:n], in1=sinT[:, 0:n], op=ALU.mult),
                  reads=[psk(7), "sinT"], writes=["rt2"])
            sc.op("dve", lambda e: e.tensor_tensor(out=dst, in0=rt1[:, 0:n], in1=rt2[:, 0:n], op=ALU.add),
                  reads=["rt1", "rt2"], writes=["nsaT"])

        def nsa_groups(st):
            groups = []
            uT = uTb[st % 2]
            cols = slice(st * 512, (st + 1) * 512)
            groups.append(lambda: rope_tables(pos_b[0:1, st * 512:(st + 1) * 512][0], 512, "t"))

            def g_fm(cc):
                bank = 2 + cc % 2
                if cc < 6:
                    dst = qn[:, cc, cols] if cc < 4 else (ksT[:, cols] if cc == 4 else kwT[:, cols])
                    proj_fm(Wn, cc * 128, (lambda pb: rope_evac(pb, bank, dst, 512)), st, bank)
                else:
                    dst = kcr[:, cols] if cc == 6 else vcr[:, cols]

                    def ev(pb):
                        sc.op("act", lambda e: e.copy(out=dst, in_=pb[:, :]), reads=[psk(bank)], writes=["nsaT"])
                    proj_fm(Wn, cc * 128, ev, st, bank)

            def g_tm(tl):
                t = st * 4 + tl
                bank = 4 + tl % 2
                pb = PS[bank]
                for j in range(16):
                    sc.op("pe", (lambda e, j=j: e.matmul(pb[:, 0:NNSA_TM], lhsT=uT[:, j, tl * 128:(tl + 1) * 128],
                                                         rhs=Wn[:, j, NNSA_FM:NNSA], start=(j == 0), stop=(j == 15))),
                          reads=["W", ("uT", st % 2)], writes=[psk(bank)], signal=(j == 15))
                sc.op("act", lambda e: e.copy(out=Vsw[:, t, :, 0:128], in_=pb[:, 0:256].rearrange("p (h d) -> p h d", h=2)),
                      reads=[psk(bank)], writes=["Vsw"])
                gk = ("gat", t)
                sc.op("act", lambda e: e.activation(out=gat[:, t, :], in_=pb[:, 256:262], func=AF.Exp, scale=-1.0),
                      reads=[psk(bank)], writes=[gk])
                sc.op("dve", lambda e: e.tensor_scalar(out=gat[:, t, :], in0=gat[:, t, :], scalar1=1.0, scalar2=None, op0=ALU.add),
                      reads=[gk], writes=[gk])
                sc.op("dve", lambda e: e.reciprocal(out=gat[:, t, :], in_=gat[:, t, :]), reads=[gk], writes=[gk, "gat"])
            for cc in (6, 7):
                groups.append(lambda cc=cc: g_fm(cc))
            for tl in range(4):
                groups.append(lambda tl=tl: g_tm(tl))
            for cc in range(6):
                groups.append(lambda cc=cc: g_fm(cc))
            return groups
        run_projection(nsa_groups, after_prologue=fox_exchange)
        sc.barrier()
        mem.off = P_MARK
        KcT = mem.t([256], BF16)
        Vca = mem.t([2, 193], BF16)
        C_MARK = mem.off
        cpT = mem.t([32], F32)
        kcp = mem.t([32, 256], BF16)
        w1 = mem.t([32, 256], BF16)
        w2 = mem.t([2, 128], BF16)
        hT = mem.t([2, 256], F32)
        hg = mem.t([2, 256], BF16)
        gt1 = mem.t([256], F32)
        gt2 = mem.t([256], F32)
        assert mem.off <= W_MARK, (mem.off, W_MARK)
        s_c = dsem("cmp", True)
        dma("sp", cpT, cposT, s_c, writes=["cpT"])
        dma("pool", Vca[:, :, 128:192], c_ovl.rearrange("p (a b) -> p a b", a=2), s_c, writes=["Vca"])
        sc.op("pool", lambda e: e.memset(Vca[:, :, 192:193], 1.0), writes=["Vca"])
        sc.op("pool", lambda e: e.memset(kcp, 0.0), writes=["kcp"])
        rope_tables(pos_end[0], 256, "e")
        s_w1 = dsem("w1", True)
        for which in range(2):
            src = kcr if which == 0 else vcr
            dma("pool", w1, (w_kc1 if which == 0 else w_vc1).rearrange("(l p) n -> p l n", p=128), s_w1, writes=["w1"])
            dma("pool", w2, (w_kc2 if which == 0 else w_vc2).rearrange("(c p) n -> p c n", p=128), s_w1, writes=["w2"])
            srcv = src.rearrange("p (c l) -> p l c", l=16)
            for l in range(32):
                lo, sh = l % 16, l // 16
                sc.op("dve", (lambda e, l=l, lo=lo, sh=sh, srcv=srcv: e.tensor_scalar(
                    out=kcp[:, l, 0:255], in0=srcv[:, lo, sh:sh + 255], scalar1=cpT[:, l:l + 1], scalar2=None, op0=ALU.add)),
                    reads=["nsaT", "cpT"], writes=["kcp"])
            for ncn in range(2):
                for l in range(32):
                    sc.op("pe", (lambda e, ncn=ncn, l=l: e.matmul(PS[ncn][:, 0:256], lhsT=w1[:, l, ncn * 128:(ncn + 1) * 128], rhs=kcp[:, l, :],
                                                                  start=(l == 0), stop=(l == 31))),
                          reads=["w1", "kcp"], writes=[psk(ncn)], signal=(l == 31))
                sc.op("act", (lambda e, ncn=ncn: e.copy(out=hT[:, ncn, :], in_=PS[ncn][:, 0:256])), reads=[psk(ncn)], writes=["hT"])
                sc.op("dve", (lambda e, ncn=ncn: e.tensor_tensor(out=gt1, in0=hT[:, ncn, :], in1=hT[:, ncn, :], op=ALU.mult)),
                      reads=["hT"], writes=["gt1"])
                sc.op("dve", lambda e: e.tensor_scalar(out=gt1, in0=gt1, scalar1=0.044715, scalar2=1.0, op0=ALU.mult, op1=ALU.add),
                      reads=["gt1"], writes=["gt1"])
                sc.op("dve", (lambda e, ncn=ncn: e.tensor_tensor(out=gt1, in0=gt1, in1=hT[:, ncn, :], op=ALU.mult)),
                      reads=["gt1", "hT"], writes=["gt1"])
                sc.op("act", lambda e: e.activation(out=gt2, in_=gt1, func=AF.Sigmoid, scale=1.5957691216), reads=["gt1"], writes=["gt2"])
                sc.op("dve", (lambda e, ncn=ncn: e.tensor_tensor(out=hg[:, ncn, :], in0=gt2, in1=hT[:, ncn, :], op=ALU.mult)),
                      reads=["gt2", "hT"], writes=["hg"])
            if which == 0:
                for ncn in range(2):
                    sc.op("pe", (lambda e, ncn=ncn: e.matmul(PS[2][:, 0:256], lhsT=w2[:, ncn, :], rhs=hg[:, ncn, :],
                                                             start=(ncn == 0), stop=(ncn == 1))),
                          reads=["w2", "hg"], writes=[psk(2)], signal=(ncn == 1))
                rope_evac(PS[2], 2, KcT[:, 0:256], 256)
                sc.op("dve", lambda e: e.memset(KcT[:, 255:256], 0.0), reads=["nsaT"], writes=["nsaT"])
            else:
                for ch in range(2):
                    for ncn in range(2):
                        sc.op("pe", (lambda e, ncn=ncn, ch=ch: e.matmul(PS[3][:, 0:128], lhsT=hg[:, ncn, ch * 128:(ch + 1) * 128], rhs=w2[:, ncn, :],
                                                                        start=(ncn == 0), stop=(ncn == 1))),
                              reads=["w2", "hg"], writes=[psk(3)], signal=(ncn == 1))
                    sc.op("act", (lambda e, ch=ch: e.copy(out=Vca[:, ch, 0:128], in_=PS[3][:, 0:128])), reads=[psk(3)], writes=["Vca"])
        sc.barrier()
        mem.off = C_MARK
        EE = mem.t([S], BF16)
        mkc = mem.t([17, 128], BF16)
        sA = mem.t([32, 64], F32)
        sB = mem.t([32, 64], F32)
        pT = mem.t([8, 128], BF16)
        pc = mem.t([2, 512], BF16)
        oc = [mem.t([4, 2, 128], F32) for _ in range(2)]
        imp = mem.t([64], F32)
        scr = mem.t([64], F32)
        wk = mem.t([64], F32)
        m8 = mem.t([8], F32)
        m8b = mem.t([8], F32)
        nsel = mem.t([64], BF16)
        nselT = [mem.t([512], BF16) for _ in range(2)]
        cf = mem.t([16], F32)
        assert mem.off <= W_MARK, (mem.off, W_MARK)
        s_a = dsem("att", True)
        dma("pool", EE[0:64, :], c_E, s_a, writes=["EE"])
        dma("pool", mkc, c_maskc.rearrange("p (a b) -> p a b", a=17), s_a, writes=["mkc"])
        dma("sp", sA, c_selA.rearrange("p (a b) -> p a b", a=32), s_a, writes=["sA"])
        dma("sp", sB, c_selB.rearrange("p (a b) -> p a b", a=32), s_a, writes=["sB"])
        pcount = [0]
        scount = [0]

        def recip_col(dst, src_ap, bank, key):
            sc.op("dve", lambda e: e.tensor_scalar(out=dst, in0=src_ap, scalar1=1e-30, scalar2=None, op0=ALU.max),
                  reads=[psk(bank)], writes=[key])
            sc.op("dve", lambda e: e.reciprocal(out=dst, in_=dst), reads=[key], writes=[key])

        def comp_gen(I):
            par = I % 2
            for iq in range(4):
                i = 4 * I + iq
                chunks = [0] if i < 16 else [0, 1]
                qv = qn[:, :, i * 128:(i + 1) * 128]
                for ch in chunks:
                    sc.op("pe", (lambda e, ch=ch, qv=qv: e.matmul(PS[2][:, :].rearrange("p (h q) -> p h q", h=4),
                                                                  lhsT=KcT[:, ch * 128:(ch + 1) * 128], rhs=qv, start=True, stop=True)),
                          reads=["KcT", "nsaT"], writes=[psk(2)])
                    sc.op("act", (lambda e, ch=ch: e.activation(out=pc[:, ch, :], in_=PS[2][:, :], func=AF.Exp, scale=SCALE)),
                          reads=[psk(2)], writes=[("pc", ch)])
                    ip = i - 16 * ch
                    if ip <= 16:
                        mb = mkc[:, ip, :].unsqueeze(1).to_broadcast([128, 4, 128])
                        sc.op("dve", (lambda e, ch=ch, mb=mb: e.tensor_tensor(out=pc[:, ch, :].rearrange("p (h q) -> p h q", h=4),
                                                                              in0=pc[:, ch, :].rearrange("p (h q) -> p h q", h=4),
                                                                              in1=mb, op=ALU.mult)),
                              reads=[("pc", ch), "mkc"], writes=[("pc", ch)])
                    yield
                for rp in range(2):
                    for r in (2 * rp, 2 * rp + 1):
                        for ci, ch in enumerate(chunks):
                            sc.op("pe", (lambda e, r=r, ch=ch, ci=ci, chunks=chunks: e.matmul(
                                PS[3][:, (r % 2) * 193:(r % 2) * 193 + 193], lhsT=pc[:, ch, r * 128:(r + 1) * 128], rhs=Vca[:, ch, :],
                                start=(ci == 0), stop=(ci == len(chunks) - 1))),
                                reads=[("pc", ch), "Vca"], writes=[psk(3)], signal=(ci == len(chunks) - 1))
                    yield
                    for r in (2 * rp, 2 * rp + 1):
                        bank = 3
                        base = (r % 2) * 193
                        rc = cf[:, r:r + 1]
                        recip_col(rc, PS[bank][:, base + 192:base + 193], bank, ("cf", r))
                        if r == 0:
                            sc.op("dve", (lambda e, bank=bank, base=base, rc=rc: e.tensor_scalar(out=imp, in0=PS[bank][:, base + 128:base + 192],
                                                                                                 scalar1=rc, scalar2=None, op0=ALU.mult)),
                                  reads=[psk(bank), ("cf", r)], writes=["imp"])
                        else:
                            sc.op("dve", (lambda e, bank=bank, base=base, rc=rc: e.scalar_tensor_tensor(out=imp, in0=PS[bank][:, base + 128:base + 192],
                                                                                                        scalar=rc, in1=imp, op0=ALU.mult, op1=ALU.add)),
                                  reads=[psk(bank), ("cf", r), "imp"], writes=["imp"])
                        if r < 2:
                            gc = cf[:, 8 + r:9 + r]
                            sc.op("dve", (lambda e, gc=gc, rc=rc, i=i, r=r: e.tensor_tensor(out=gc, in0=rc, in1=gat[:, i, 3 * r:3 * r + 1], op=ALU.mult)),
                                  reads=[("cf", r), "gat"], writes=[("cf", 8 + r)])
                            sc.op("dve", (lambda e, bank=bank, base=base, gc=gc, iq=iq, r=r, par=par: e.tensor_scalar(
                                out=oc[par][:, iq, r, :], in0=PS[bank][:, base:base + 128], scalar1=gc, scalar2=None, op0=ALU.mult)),
                                reads=[psk(bank), ("cf", 8 + r)], writes=[("oc", par, iq, r)])
                sc.op("dve", (lambda e, i=i: e.tensor_tensor(out=scr, in0=imp, in1=sA[:, i, :], op=ALU.mult)), reads=["imp", "sA"], writes=["scr"])
                sc.op("dve", (lambda e, i=i: e.tensor_tensor(out=scr, in0=scr, in1=sB[:, i, :], op=ALU.add)), reads=["scr", "sB"], writes=["scr"])
                sc.op("dve", lambda e: e.max(out=m8, in_=scr), reads=["scr"], writes=["m8"])
                sc.op("dve", lambda e: e.match_replace(out=wk, in_to_replace=m8, in_values=scr, imm_value=-3.0e38), reads=["m8", "scr"], writes=["wk"])
                sc.op("dve", lambda e: e.max(out=m8b, in_=wk), reads=["wk"], writes=["m8b"])
                sc.op("dve", lambda e: e.tensor_scalar(out=wk, in0=scr, scalar1=m8b[:, 7:8], scalar2=None, op0=ALU.is_ge),
                      reads=["scr", "m8b"], writes=["wk"])
                sc.op("dve", lambda e: e.tensor_scalar(out=nsel, in0=wk, scalar1=-1.0, scalar2=BIG, op0=ALU.add, op1=ALU.mult),
                      reads=["wk"], writes=["nsel"])
                yield
                pbv = PS[2][:, :].bitcast(BF16)
                sc.op("pe", (lambda e, pbv=pbv: e.transpose(pbv[0:64, 0:128], nsel, ident)), reads=["nsel", "ident"], writes=[psk(2)])
                sc.op("act", (lambda e, pbv=pbv, iq=iq, par=par: e.copy(out=nselT[par][0:64, iq * 128:(iq + 1) * 128], in_=pbv[0:64, 0:128])),
                      reads=[psk(2)], writes=[("nselT", par)])
                yield

        def selwin(I, gen):
            par = I % 2
            steps = []
            for br in range(2):
                jlo = 0 if br == 0 else max(0, 4 * I - 4)
                for m in range(2):
                    for j in range(jlo, 4 * I + 4):
                        steps.append((br, m, j, jlo))

            def qk(n):
                br, m, j, jlo = steps[n]
                sb = n % 2
                kT_ = ksT if br == 0 else kwT
                sc.op("pe", (lambda e, sb=sb, m=m, j=j, kT_=kT_, br=br, I=I: e.matmul(PS[sb][:, :], lhsT=kT_[:, j * 128:(j + 1) * 128],
                                                                                 rhs=qn[:, m, I * 512:(I + 1) * 512], start=True, stop=(br == 1))),
                      reads=["nsaT"], writes=[psk(sb)], signal=(br == 1))
                if br == 0:
                    sc.op("pe", (lambda e, sb=sb, j=j, par=par: e.matmul(PS[sb][:, :], lhsT=EE[0:64, j * 128:(j + 1) * 128], rhs=nselT[par][0:64, :],
                                                                         start=False, stop=True)),
                          reads=["EE", ("nselT", par)], writes=[psk(sb)])
            qk(0)
            for n, (br, m, j, jlo) in enumerate(steps):
                if n + 1 < len(steps):
                    qk(n + 1)
                sb = n % 2
                for iq in range(4):
                    i = 4 * I + iq
                    if j > i or (br == 1 and j < i - 4):
                        continue
                    slot = pcount[0] % 8
                    pcount[0] += 1
                    sc.op("act", (lambda e, sb=sb, slot=slot, iq=iq: e.activation(out=pT[:, slot, :], in_=PS[sb][:, iq * 128:(iq + 1) * 128],
                                                                                  func=AF.Exp, scale=SCALE)),
                          reads=[psk(sb)], writes=[("pT", slot)])
                    msk = tri if j == i else (atri if (br == 1 and j == i - 4) else None)
                    if msk is not None:
                        sc.op("dve", (lambda e, slot=slot, msk=msk: e.tensor_tensor(out=pT[:, slot, :], in0=pT[:, slot, :], in1=msk, op=ALU.mult)),
                              reads=[("pT", slot), "tri", "atri"], writes=[("pT", slot)])
                    first = (j == jlo) if br == 0 else (j == max(0, i - 4))
                    sc.op("pe", (lambda e, slot=slot, iq=iq, j=j, br=br, first=first, i=i: e.matmul(
                        PS[4 + iq][:, 0:130], lhsT=pT[:, slot, :], rhs=Vsw[:, j, br, :], start=first, stop=(j == i))),
                        reads=[("pT", slot), "Vsw"], writes=[psk(4 + iq)], signal=(j == i))
                    if j == i:
                        rc = cf[:, 4 + iq:5 + iq]
                        recip_col(rc, PS[4 + iq][:, 128:129], 4 + iq, ("cf", 4 + iq))
                        sc.op("dve", (lambda e, rc=rc, i=i, m=m, br=br: e.tensor_tensor(out=rc, in0=rc, in1=gat[:, i, 3 * m + 1 + br:3 * m + 2 + br], op=ALU.mult)),
                              reads=[("cf", 4 + iq), "gat"], writes=[("cf", 4 + iq)])
                        sc.op("dve", (lambda e, rc=rc, iq=iq, m=m, par=par: e.scalar_tensor_tensor(out=oc[par][:, iq, m, :], in0=PS[4 + iq][:, 0:128], scalar=rc,
                                                                                                   in1=oc[par][:, iq, m, :], op0=ALU.mult, op1=ALU.add)),
                              reads=[psk(4 + iq), ("cf", 4 + iq), ("oc", par, iq, m)], writes=[("oc", par, iq, m)])
                        if br == 1:
                            sc.op("act", (lambda e, iq=iq, m=m, i=i, par=par: e.copy(out=ostage[:, i, m * 128:(m + 1) * 128], in_=oc[par][:, iq, m, :])),
                                  reads=[("oc", par, iq, m)], writes=[("ostage", i // 8)])
                if gen is not None:
                    next(gen, None)
            if gen is not None:
                for _ in gen:
                    pass

        for _ in comp_gen(0):
            pass
        s_on = dsem("osrc_n", True)
        s_cc2 = sc.new_sem("cc1", True)
        for I in range(8):
            selwin(I, comp_gen(I + 1) if I < 7 else None)
            if I % 2 == 1:
                q = I // 2
                dma("sp", o_srcn[q].rearrange("(t p) f -> p t f", p=128), ostage[:, 8 * q:8 * q + 8, :], s_on,
                    reads=[("ostage", q)], writes=[("o_srcn", q)])
                if stage >= 3:
                    sc.op("pool", (lambda e, q=q: e.collective_compute("AllGather", ALU.bypass, replica_groups=RG,
                                                                       ins=[o_srcn[q]], outs=[o_alln[q * S:(q + 1) * S, :]])),
                          reads=[("o_srcn", q)], writes=[("o_alln", q)], dsem=s_cc2, inc=1)
        sc.barrier()
    mem.off = P_MARK

    if stage < 3:
        dbg = nc.dram_tensor("dbg_o", [S, 512], BF16, kind="ExternalOutput").ap()
        dbgm = nc.dram_tensor("dbg_mod", [4, 3072], F32, kind="ExternalOutput").ap()
        mem.off = P_MARK
        stg = mem.t([3072], F32)
        s_d = dsem("dbg", True)
        ncol = 512 if stage >= 2 else 256
        for q in range(4):
            dma("sp", dbg[q * 1024:(q + 1) * 1024, 0:256], o_srcf[q], s_d, reads=["o_srcf"], writes=["dbg"])
            if stage >= 2:
                dma("sp", dbg[q * 1024:(q + 1) * 1024, 256:512], o_srcn[q], s_d, reads=[("o_srcn", q)], writes=["dbg"])
        dma("sp", stg[0:4, :], mod_all, s_d, reads=["mod_all"], writes=["stg"])
        dma("sp", dbgm, stg[0:4, :], s_d, reads=["stg"], writes=["dbgm"])
    else:
        mem.off = P_MARK - 16 * 1024 - 512
        small2 = mem.t([128], F32)
        EPS2 = small2[:, 120:121]
        g1B = mem.t([D], F32)
        u2T = mem.t([16, 1024], BF16)
        R_MARK = mem.off
        w_o = mem.t([16, D], BF16)
        yt_all = mem.t([8, D], BF16)
        ytn = [mem.t([D], BF16) for _ in range(2)]
        yT = [mem.t([16, 128], BF16) for _ in range(2)]
        xt = [mem.t([D], F32) for _ in range(2)]
        h1 = [mem.t([D], F32) for _ in range(2)]
        h1n = [mem.t([D], BF16) for _ in range(2)]
        ut2 = mem.t([1024], F32)
        s_p2 = dsem("p2", True)
        mflat = mod_all.rearrange("r n -> (r n)")
        dma("sp", g1B, mflat[4096:6144].partition_broadcast(128), s_p2, reads=["mod_all"], writes=["g1B", "ostage"])
        sc.op("dve", lambda e: e.memset(EPS2, EPS), writes=["eps2", "ostage"])
        sc.op("dve", lambda e: e.memset(small2[:, 0:64], 0.0), writes=["ss2all"])
        s_wo = dsem("wo", True)
        w_out_v = w_out.rearrange("(j p) n -> p j n", p=128)
        for dc in range(4):
            dma("pool", w_o[:, :, dc * 512:(dc + 1) * 512], w_out_v[:, :, dc * 512:(dc + 1) * 512], dsem("wo%d" % dc), writes=[("w_o", dc)])
        s_y = dsem("y")
        s_x = [dsem("x2a"), dsem("x2b")]
        s_h = [dsem("h1a"), dsem("h1b")]

        def rstd_act(ssx, rsx, n, kin, kout):
            sc.op("act", lambda e: e.activation(out=rsx, in_=ssx, func=AF.Ln, scale=1.0 / n, bias=EPS2), reads=[kin, "eps2"], writes=[kout])
            sc.op("act", lambda e: e.activation(out=rsx, in_=rsx, func=AF.Exp, scale=-0.5), reads=[kout], writes=[kout])

        def rstd_from(ssx, rsx, n, tag):
            rstd_act(ssx, rsx, n, tag + "ss", tag + "rs")

        sc.want_pid = True
        for r in range(4):
            for a in range(2):
                def ld_y(e, r=r, a=a):
                    src_all = o_allf if a == 0 else o_alln
                    return e.dma_start(out=yt_all[:, :, a * 1024 + r * 256:a * 1024 + (r + 1) * 256],
                                       in_=src_all[bass.ds(sc.rank_row + r * 1024, 1024), :].rearrange("(t p) c -> p t c", p=128))
                sc.op("pool", ld_y, reads=[("o_allf", q) for q in range(4)] + [("o_alln", q) for q in range(4)], writes=["yt"], dsem=s_y)

        def S1(n):
            p = n % 2
            yt = yt_all[:, n, :]
            for hf in range(2):
                ssx = small2[:, 2 * n + hf:2 * n + hf + 1]
                rsx = small2[:, 16 + 2 * n + hf:16 + 2 * n + hf + 1]
                cs = slice(hf * 1024, (hf + 1) * 1024)
                kss, krs = ("yss", n, hf), ("yrs", n, hf)
                sc.op("act", (lambda e, cs=cs, ssx=ssx: e.activation(out=ytn[p][:, cs], in_=yt[:, cs], func=AF.Square, accum_out=ssx)),
                      reads=["yt", "ss2all"], writes=[("ytn", p), kss])
                rstd_act(ssx, rsx, 1024, kss, krs)
                sc.op("act", (lambda e, cs=cs, rsx=rsx: e.activation(out=ytn[p][:, cs], in_=yt[:, cs], func=AF.Copy, scale=rsx)),
                      reads=["yt", krs], writes=[("ytn", p)])

        def transposes(src, tagsrc, evac):
            for half in range(2):
                pbv = PS[half][:, :].bitcast(BF16)
                for jj in range(8):
                    j = half * 8 + jj
                    sc.op("pe", (lambda e, pbv=pbv, jj=jj, j=j: e.transpose(pbv[:, jj * 128:(jj + 1) * 128],
                                                                            src[:, j * 128:(j + 1) * 128], ident)),
                          reads=[tagsrc, "ident"], writes=[psk(half)], signal=(jj == 7))
                evac(half, pbv.rearrange("p (a b) -> p a b", a=8))

        def S2(n):
            p = n % 2

            def ev_y(half, pv3):
                bb = betaT[:, half * 8:(half + 1) * 8].unsqueeze(2).to_broadcast([128, 8, 128])
                sc.op("dve", lambda e: e.tensor_tensor(out=yT[p][:, half * 8:(half + 1) * 8, :], in0=pv3, in1=bb, op=ALU.mult),
                      reads=[psk(half), "betaT"], writes=[("yT", p)])
            transposes(ytn[p], ("ytn", p), ev_y)

        def S3(n):
            p = n % 2
            for dc in range(4):
                for j in range(16):
                    sc.op("pe", (lambda e, dc=dc, j=j: e.matmul(PS[4 + dc][:, :], lhsT=yT[p][:, j, :], rhs=w_o[:, j, dc * 512:(dc + 1) * 512],
                                                                start=(j == 0), stop=(j == 15))),
                          reads=[("yT", p), ("w_o", dc)], writes=[psk(4 + dc)], signal=(j == 15))

        def S4(n):
            p = n % 2
            for dc in range(4):
                cs = slice(dc * 512, (dc + 1) * 512)
                sc.op("dve", (lambda e, dc=dc, cs=cs: e.tensor_tensor(out=h1[p][:, cs], in0=PS[4 + dc][:, :], in1=g1B[:, cs], op=ALU.mult)),
                      reads=[psk(4 + dc), "g1B"], writes=[("h1", p, dc)])
                sc.op("dve", (lambda e, cs=cs: e.tensor_tensor(out=h1[p][:, cs], in0=h1[p][:, cs], in1=xt[p][:, cs], op=ALU.add)),
                      reads=[("h1", p, dc), ("xt", p)], writes=[("h1", p, dc)])
            h1k = [("h1", p, dc) for dc in range(4)]
            dma("sp", h1_scr[n * 128:(n + 1) * 128, :], h1[p], s_h[p], reads=h1k, writes=[("h1scr", n)])

        def S5(n):
            p = n % 2
            h1k = [("h1", p, dc) for dc in range(4)]
            ssx = small2[:, 32 + n:33 + n]
            rsx = small2[:, 48 + n:49 + n]
            sc.op("act", lambda e: e.activation(out=h1n[p], in_=h1[p], func=AF.Square, accum_out=ssx),
                  reads=h1k + ["ss2all"], writes=[("h1n", p), ("hss", n)])
            rstd_act(ssx, rsx, D, ("hss", n), ("hrs", n))
            sc.op("act", lambda e: e.activation(out=h1n[p], in_=h1[p], func=AF.Copy, scale=rsx), reads=h1k + [("hrs", n)], writes=[("h1n", p)])

        def S6(n):
            p = n % 2

            def ev_u(half, pv3):
                a2b = A2[:, half * 8:(half + 1) * 8].unsqueeze(2).to_broadcast([128, 8, 128])
                b2b = B2[:, half * 8:(half + 1) * 8].unsqueeze(2).to_broadcast([128, 8, 128])
                ut3 = ut2.rearrange("p (a b) -> p a b", a=8)
                sc.op("dve", lambda e: e.tensor_tensor(out=ut3, in0=pv3, in1=a2b, op=ALU.mult), reads=[psk(half), "A2"], writes=["ut2"])
                sc.op("dve", lambda e: e.tensor_tensor(out=u2T[:, half * 8:(half + 1) * 8, n * 128:(n + 1) * 128], in0=ut3, in1=b2b, op=ALU.add),
                      reads=["ut2", "B2"], writes=["u2T"])
            transposes(h1n[p], ("h1n", p), ev_u)

        def ldx(n):
            dma("sp", xt[n % 2], x_my[n * 128:(n + 1) * 128, :], s_x[n % 2], writes=[("xt", n % 2)])
        ldx(0)
        ldx(1)
        S1(0)
        S1(1)
        S2(0)
        for n in range(8):
            if n + 2 < 8:
                S1(n + 2)
            if n + 1 < 8:
                S2(n + 1)
            S3(n)
            if n >= 1:
                S6(n - 1)
            S4(n)
            if n + 2 < 8:
                ldx(n + 2)
            S5(n)
        S6(7)
        def p2_tail():
            sc.barrier()
            mem.off = R_MARK
            aT = mem.t([NFC, 1024], BF16)
            g2B = mem.t([D], F32)
            fnB = mem.t([D], F32)
            s_p3 = dsem("p3", True)
            dma("sp", g2B, mflat[10240:12288].partition_broadcast(128), s_p3, reads=["mod_all"], writes=["g2B"])
            dma("sp", fnB, fnorm[0].partition_broadcast(128), s_p3, writes=["fnB"])
            WG_OFF = mem.off
            wg = [mem.t([16, 128], BF16) for _ in range(2)]
            wu = [mem.t([16, 128], BF16) for _ in range(2)]
            sg = [mem.t([512], F32) for _ in range(2)]
            wd = [mem.t([4, 512], BF16) for _ in range(3)]
            hb = [mem.t([512], F32) for _ in range(2)]
            ob = [mem.t([512], F32) for _ in range(2)]
            orow = mem.t([D], F32)
            ss3 = mem.t([32], F32)
            s_wg = [dsem("wg0"), dsem("wg1")]
            s_wu = [dsem("wu0"), dsem("wu1")]
            wgv = w_gate.rearrange("(j p) n -> p j n", p=128)
            wuv = w_up.rearrange("(j p) n -> p j n", p=128)
            for fc in range(NFC):
                sl = fc % 2
                dma("pool", wg[sl], wgv[:, :, fc * 128:(fc + 1) * 128], s_wg[sl], writes=[("wg", sl)])
                dma("pool", wu[sl], wuv[:, :, fc * 128:(fc + 1) * 128], s_wu[sl], writes=[("wu", sl)])
                for half in range(2):
                    bg = (fc % 2) * 4 + half * 2
                    bu = bg + 1
                    for j in range(16):
                        sc.op("pe", (lambda e, bg=bg, sl=sl, j=j, half=half: e.matmul(PS[bg][:, :], lhsT=wg[sl][:, j, :],
                                                                                      rhs=u2T[:, j, half * 512:(half + 1) * 512],
                                                                                      start=(j == 0), stop=(j == 15))),
                              reads=[("wg", sl), "u2T"], writes=[psk(bg)], signal=(j == 15))
                    for j in range(16):
                        sc.op("pe", (lambda e, bu=bu, sl=sl, j=j, half=half: e.matmul(PS[bu][:, :], lhsT=wu[sl][:, j, :],
                                                                                      rhs=u2T[:, j, half * 512:(half + 1) * 512],
                                                                                      start=(j == 0), stop=(j == 15))),
                              reads=[("wu", sl), "u2T"], writes=[psk(bu)], signal=(j == 15))
                    sc.op("act", (lambda e, bg=bg, half=half: e.activation(out=sg[half], in_=PS[bg][:, :], func=AF.Silu)),
                          reads=[psk(bg)], writes=[("sg", half)])
                    sc.op("dve", (lambda e, bu=bu, half=half, fc=fc: e.tensor_tensor(out=aT[:, fc, half * 512:(half + 1) * 512], in0=sg[half],
                                                                                     in1=PS[bu][:, :], op=ALU.mult)),
                          reads=[("sg", half), psk(bu)], writes=["aT"])
            if stage < 3.3:
                return
            s_wd = [dsem("wd%d" % i) for i in range(4)]
            s_hb = [dsem("hb0"), dsem("hb1")]
            s_ob = [dsem("ob0"), dsem("ob1")]
            sc.op("dve", lambda e: e.memset(ss3, 0.0), writes=["ss3"])
            cnt = 0
            WG = 4
            gcount = 0
            for dcn in range(4):
                for fg in range(NFC // WG):
                    sl = gcount % 3
                    gcount += 1
                    dma("pool", wd[sl], w_down[fg * WG * 128:(fg + 1) * WG * 128, dcn * 512:(dcn + 1) * 512].rearrange("(g p) n -> p g n", p=128),
                        s_wd[sl], writes=[("wd", sl)])
                    for g in range(WG):
                        fc = fg * WG + g
                        for tt in range(8):
                            sc.op("pe", (lambda e, tt=tt, fc=fc, sl=sl, g=g: e.matmul(PS[tt][:, :], lhsT=aT[:, fc, tt * 128:(tt + 1) * 128], rhs=wd[sl][:, g, :],
                                                                                      start=(fc == 0), stop=(fc == NFC - 1))),
                                  reads=["aT", ("wd", sl)], writes=[psk(tt)], signal=(fc == NFC - 1 or tt == 7))
                for tt in range(8):
                    s2 = cnt % 2
                    cnt += 1
                    dma("sp", hb[s2], h1_scr[tt * 128:(tt + 1) * 128, dcn * 512:(dcn + 1) * 512], s_hb[s2],
                        reads=[("h1scr", tt)], writes=[("hb", s2)])
                    sc.op("dve", (lambda e, tt=tt, s2=s2, dcn=dcn: e.tensor_tensor(out=ob[s2], in0=PS[tt][:, :],
                                                                                   in1=g2B[:, dcn * 512:(dcn + 1) * 512], op=ALU.mult)),
                          reads=[psk(tt), "g2B"], writes=[("ob", s2)])
                    sc.op("dve", (lambda e, s2=s2: e.tensor_tensor(out=ob[s2], in0=ob[s2], in1=hb[s2], op=ALU.add)),
                          reads=[("ob", s2), ("hb", s2)], writes=[("ob", s2)])
                    sc.op("act", (lambda e, s2=s2, tt=tt, dcn=dcn: e.activation(out=hb[s2], in_=ob[s2], func=AF.Square,
                                                                                accum_out=ss3[:, tt * 4 + dcn:tt * 4 + dcn + 1])),
                          reads=[("ob", s2), "ss3"], writes=[("hb", s2), "ss3"])
                    dma("sp", out[tt * 128:(tt + 1) * 128, dcn * 512:(dcn + 1) * 512], ob[s2], s_ob[s2],
                        reads=[("ob", s2)], writes=[("outpre", tt)])
            if stage < 3.4:
                return
            orows = [orow, mem.at(WG_OFF, [D], F32), mem.at(WG_OFF + 8192, [D], F32)]
            s_ld = [dsem("orl%d" % i) for i in range(3)]
            s_st = [dsem("ors%d" % i) for i in range(3)]

            def fin_load(tt):
                dma("pool", orows[tt % 3], out[tt * 128:(tt + 1) * 128, :], s_ld[tt % 3], reads=[("outpre", tt)], writes=[("orow", tt % 3)])
            fin_load(0)
            fin_load(1)
            for tt in range(8):
                if tt + 2 < 8:
                    fin_load(tt + 2)
                ob3 = orows[tt % 3]
                ssf = small2[:, 100 + tt:101 + tt]
                rsf = small2[:, 108 + tt:109 + tt]
                sc.op("dve", (lambda e, tt=tt, ssf=ssf: e.tensor_reduce(out=ssf, in_=ss3[:, tt * 4:(tt + 1) * 4], axis=mybir.AxisListType.X, op=ALU.add)),
                      reads=["ss3"], writes=[("fss", tt)])
                rstd_act(ssf, rsf, D, ("fss", tt), ("frs", tt))
                sc.op("dve", (lambda e, ob3=ob3, rsf=rsf: e.scalar_tensor_tensor(out=ob3, in0=ob3, scalar=rsf, in1=fnB, op0=ALU.mult, op1=ALU.mult)),
                      reads=[("orow", tt % 3), ("frs", tt), "fnB"], writes=[("orow", tt % 3)])
                dma("sp", out[tt * 128:(tt + 1) * 128, :], ob3, s_st[tt % 3], reads=[("orow", tt % 3)], writes=[("out", tt)])

        if stage >= 3.2:
            p2_tail()
        if stage < 3.4:
            sc.barrier()
            s_dd = dsem("dbgh", True)
            dma("sp", out, h1_scr, s_dd, reads=[], writes=["outdbg"])

    sc.barrier()
    with nc.Block() as block:
        sc.replay(block)
    nc._in_names = in_names
    return nc


def nsa_phase(nc, sc, mem, L):
    raise NotImplementedError


def ffn_phase(nc, sc, mem, L):
    raise NotImplementedError


def _consts():
    p = np.arange(128)
    ident = np.eye(128, dtype=np.float32)
    perm = np.zeros((128, 128), np.float32)
    perm[(p + 64) % 128, p] = 1.0
    tri = (p[None, :] >= p[:, None]).astype(np.float32)
    atri = (p[:, None] > p[None, :]).astype(np.float32)
    ones = np.ones((128, 128), np.float32)
    E = np.zeros((64, S), np.float32)
    E[np.arange(S) // 64, np.arange(S)] = 1.0
    maskc = np.zeros((128, 17, 128), np.float32)
    for ip in range(17):
        maskc[:, ip, :] = (16 * p[:, None] + 31 <= 128 * ip + p[None, :])
    selA = np.zeros((128, 32, 64), np.float32)
    selB = np.zeros((128, 32, 64), np.float32)
    n = np.arange(64)
    for i in range(32):
        cur = (128 * i + p) // 64
        forced = (n[None, :] == 0) | (n[None, :] == cur[:, None]) | (n[None, :] == cur[:, None] - 1)
        causal = n[None, :] <= cur[:, None]
        selA[:, i, :] = (causal & ~forced)
        selB[:, i, :] = np.where(causal, np.where(forced, 1e6, 0.0), -1e6)
    c = np.arange(256)
    ovl = (np.minimum(c[:, None] * 16 + 32, n[None, :] * 64 + 64) - np.maximum(c[:, None] * 16, n[None, :] * 64))
    ovl = np.clip(ovl, 0, None).astype(np.float32) / 16.0
    ovl[255] = 0.0
    ovl = ovl.reshape(2, 128, 64).transpose(1, 0, 2)
    freq = np.zeros((128, 4), np.float32)
    inv = (10000.0 ** (-np.arange(0, 128, 2, dtype=np.float32) / 128)).astype(np.float32)
    freq[:, 0] = np.concatenate([inv, inv])
    freq[:, 1] = np.concatenate([-np.ones(64), np.ones(64)])
    freq[:, 2] = np.pi
    return dict(c_ident=ident, c_perm=perm, c_tri=tri, c_atri=atri, c_triU=tri.copy(), c_ones=ones, c_E=E,
                c_maskc=maskc.reshape(128, -1), c_selA=selA.reshape(128, -1), c_selB=selB.reshape(128, -1),
                c_ovl=np.ascontiguousarray(ovl).reshape(128, -1), c_freq=freq)


def _tm(v):
    return np.ascontiguousarray(np.asarray(v).reshape(16, 128).T)


def make_in_maps(x, c, positions, w_ada, b_ada, norm_attn, norm_ffn, w_in, b_fgate, cmp_pos,
                 w_kc1, w_kc2, w_vc1, w_vc2, beta_fox, beta_nsa, w_out, w_gate, w_up, w_down, final_norm):
    f = lambda a: np.ascontiguousarray(np.asarray(a))
    x = f(x); c = f(c); positions = f(positions).astype(np.int32)
    w_in0 = f(w_in)[0]
    consts = _consts()
    shared = dict(
        nattn_t=_tm(f(norm_attn)[0]), nffn_t=_tm(f(norm_ffn)[0]), fnorm=f(final_norm).reshape(1, D),
        cposT=np.ascontiguousarray(f(cmp_pos)[0].T), w_kc1=f(w_kc1)[0], w_kc2=f(w_kc2)[0], w_vc1=f(w_vc1)[0], w_vc2=f(w_vc2)[0],
        beta_t=_tm(np.concatenate([f(beta_fox)[0], f(beta_nsa)[0]])),
        w_out=f(w_out)[0], w_gate=f(w_gate)[0], w_up=f(w_up)[0], w_down=f(w_down)[0], **consts)
    maps = []
    QF, KF, VF, FL, QN, KC, VC, KS, VS, KW, VW, GN = 0, 1024, 2048, 3072, 3080, 4104, 4360, 4616, 4872, 5128, 5384, 5640
    cmp_end = np.arange(255) * 16 + 31
    for core in range(8):
        b, k = core // 4, core % 4
        g = k // 2
        hs = [2 * k, 2 * k + 1]
        oth = [h for h in range(4 * g, 4 * g + 4) if h not in hs]
        cols = []
        for base in (QF, KF, VF):
            for h in hs:
                cols.append(np.arange(base + h * 128, base + (h + 1) * 128))
        cols.append(np.array([FL + hs[0], FL + hs[1]]))
        w_fox = np.ascontiguousarray(w_in0[:, np.concatenate(cols)])
        cols = []
        for h in hs + oth:
            cols.append(np.arange(QN + h * 128, QN + (h + 1) * 128))
        for base in (KS, KW, KC, VC, VS, VW):
            cols.append(np.arange(base + g * 128, base + (g + 1) * 128))
        for h in hs:
            cols.append(np.arange(GN + h * 3, GN + h * 3 + 3))
        w_nsa = np.ascontiguousarray(w_in0[:, np.concatenate(cols)])
        pe = np.zeros((1, 256), np.int32)
        pe[0, :255] = positions[b, cmp_end]
        m = dict(shared)
        m.update(
            x_b=x[b], x_my=np.ascontiguousarray(x[b, 1024 * k:1024 * (k + 1)]), c_t=_tm(c[b]),
            pos_b=positions[b].reshape(1, S), pos_end=pe,
            w_ada=np.ascontiguousarray(f(w_ada)[0][:, 3072 * k:3072 * (k + 1)]),
            b_ada=np.ascontiguousarray(f(b_ada)[0][3072 * k:3072 * (k + 1)]).reshape(1, 3072),
            w_fox=w_fox, b_fg=f(b_fgate)[0][hs].reshape(1, 2).astype(np.float32), w_nsa=w_nsa)
        maps.append(m)
    return maps


_NC_CACHE = {}


def kernel(**inputs):
    maps = make_in_maps(**inputs)
    if "nc" not in _NC_CACHE:
        _NC_CACHE["nc"] = build()
    nc = _NC_CACHE["nc"]
    maps = [{k: m[k] for k in nc._in_names} for m in maps]
    res = run_bass_kernel_spmd(nc, maps, core_ids=list(range(8)))
    outp = np.zeros((2, S, D), np.float32)
    for core in range(8):
        b, k = core // 4, core % 4
        outp[b, 1024 * k:1024 * (k + 1)] = res.results[core]["out"]
    return outp
```
